# Optimizing a Trainium2 kernel written in Bass

```python
import jax, jax.numpy as jnp
from jax import lax
import numpy as np

D_MODEL = 1024
BATCH = 8
SEQ = 2048
DEPTH = 2
DEC_BATCH = 128
DEC_SEQ = 1
PAST_LEN = 16384
PAGE_SIZE = 128

N_MIXERS = 2
N_POOL_LAYERS = (DEPTH + N_MIXERS - 1) // N_MIXERS
N_CONV_LAYERS = DEPTH // N_MIXERS
POOL_WINDOWS = (2, 4, 8, 16)
N_POOL_GROUPS = len(POOL_WINDOWS)
POOL_GROUP = D_MODEL // N_POOL_GROUPS
POOL_STATE = max(POOL_WINDOWS) - 1
CONV_WIDTH = 3
CONV_STATE = CONV_WIDTH - 1
D_CONV = D_MODEL
D_FF = 4 * D_MODEL
N_MEM = 256
MEM_HEADS = 4
MEM_HEAD_DIM = D_MODEL // MEM_HEADS
EPS = 1e-6

kernel_name = "hybrid_pool_shortconv_memxattn_step"


def rmsnorm(x, g):
    xf = x.astype(jnp.float32)
    y = xf * lax.rsqrt(jnp.mean(xf * xf, axis=-1, keepdims=True) + EPS)
    return (y * g.astype(jnp.float32)).astype(x.dtype)


def pool_mixer(u, prev, start_pos, w_pool, scale):
    b, t, d = u.shape
    ext = jnp.concatenate([prev, u], axis=1)
    c = jnp.cumsum(ext.astype(jnp.float32), axis=1)
    c = jnp.pad(c, ((0, 0), (1, 0), (0, 0)))
    end = c[:, POOL_STATE + 1:POOL_STATE + 1 + t]
    pos = start_pos + jnp.arange(t)
    means = []
    for g, w in enumerate(POOL_WINDOWS):
        sl = slice(g * POOL_GROUP, (g + 1) * POOL_GROUP)
        begin = c[:, POOL_STATE + 1 - w:POOL_STATE + 1 - w + t, sl]
        cnt = jnp.minimum(pos + 1, w).astype(jnp.float32)[None, :, None]
        means.append((end[..., sl] - begin) / cnt)
    pooled = jnp.concatenate(means, axis=-1) - u.astype(jnp.float32)
    pooled = pooled.reshape(b, t, N_POOL_GROUPS, POOL_GROUP).astype(u.dtype)
    y = jnp.einsum('btgc,gce->btge', pooled, w_pool).reshape(b, t, d)
    return y * scale, ext[:, -POOL_STATE:]


def conv_mixer(u, prev, w_in, w_conv, w_out):
    t = u.shape[1]
    bch = jnp.einsum('btd,de->bte', u, w_in)
    gate_b, gate_c, h = jnp.split(bch, 3, axis=-1)
    ext = jnp.concatenate([prev, gate_c * h], axis=1)
    conv = ext[:, 0:t] * w_conv[0]
    for k in range(1, CONV_WIDTH):
        conv = conv + ext[:, k:k + t] * w_conv[k]
    y = jnp.einsum('bte,ed->btd', gate_b * conv, w_out)
    return y, ext[:, -CONV_STATE:]


def mem_kv(mem, g_mem, w_kv):
    b, n, _ = mem.shape
    kv = jnp.einsum('bnd,de->bne', rmsnorm(mem, g_mem), w_kv)
    k, v = jnp.split(kv, 2, axis=-1)
    return (k.reshape(b, n, MEM_HEADS, MEM_HEAD_DIM), v.reshape(b, n, MEM_HEADS, MEM_HEAD_DIM))


def mem_attend(u, k, v, w_q, w_o):
    b, t, d = u.shape
    q = jnp.einsum('btd,de->bte', u, w_q).reshape(b, t, MEM_HEADS, MEM_HEAD_DIM)
    s = jnp.einsum('bthd,bnhd->bhtn', q, k).astype(jnp.float32) * (MEM_HEAD_DIM ** -0.5)
    p = jax.nn.softmax(s, axis=-1).astype(v.dtype)
    o = jnp.einsum('bhtn,bnhd->bthd', p, v).reshape(b, t, d)
    return jnp.einsum('btd,de->bte', o, w_o)


def sq_relu_mlp(u, w_up, w_down):
    h = jax.nn.relu(jnp.einsum('btd,df->btf', u, w_up))
    return jnp.einsum('btf,fd->btd', h * h, w_down)


def trunk(x, pool_prev, conv_prev, mem_k, mem_v, start_pos,
          g_mix, g_attn, g_ffn, g_final, w_pool, pool_scale,
          w_conv_in, w_conv, w_conv_out, w_q, w_o, w_up, w_down):
    new_pool, new_conv = [], []
    for i in range(DEPTH):
        j = i // N_MIXERS
        u = rmsnorm(x, g_mix[i])
        if i % N_MIXERS == 0:
            y, st = pool_mixer(u, pool_prev[j], start_pos, w_pool[j], pool_scale[j])
            new_pool.append(st)
        else:
            y, st = conv_mixer(u, conv_prev[j], w_conv_in[j], w_conv[j], w_conv_out[j])
            new_conv.append(st)
        x = x + y
        x = x + mem_attend(rmsnorm(x, g_attn[i]), mem_k[i], mem_v[i], w_q[i], w_o[i])
        x = x + sq_relu_mlp(rmsnorm(x, g_ffn[i]), w_up[i], w_down[i])
    return rmsnorm(x, g_final), jnp.stack(new_pool), jnp.stack(new_conv)


def setup_inputs(seed: int = 0) -> dict:
    key = jax.random.key(seed)
    ks = jax.random.split(key, 24)
    f32 = jnp.float32
    nrm = lambda k, shape, s: jax.random.normal(k, shape, f32) * s
    gain = lambda k, shape: 1.0 + 0.05 * jax.random.normal(k, shape, f32)
    mem_shape = (DEPTH, DEC_BATCH, N_MEM, MEM_HEADS, MEM_HEAD_DIM)
    return {
        "x_prompt": nrm(ks[0], (BATCH, SEQ, D_MODEL), 1.0),
        "x_sample": nrm(ks[1], (DEC_BATCH, DEC_SEQ, D_MODEL), 1.0),
        "state_pool": nrm(ks[2], (N_POOL_LAYERS, DEC_BATCH, POOL_STATE, D_MODEL), 1.0),
        "state_conv": nrm(ks[3], (N_CONV_LAYERS, DEC_BATCH, CONV_STATE, D_CONV), 1.0),
        "cache_mem_k": nrm(ks[4], mem_shape, 1.0),
        "cache_mem_v": nrm(ks[5], mem_shape, 1.0),
        "mem_prompt": nrm(ks[6], (BATCH, N_MEM, D_MODEL), 1.0),
        "g_mix": gain(ks[7], (DEPTH, D_MODEL)),
        "g_attn": gain(ks[8], (DEPTH, D_MODEL)),
        "g_mem": gain(ks[9], (DEPTH, D_MODEL)),
        "g_ffn": gain(ks[10], (DEPTH, D_MODEL)),
        "g_final": gain(ks[11], (D_MODEL,)),
        "w_pool": nrm(ks[12], (N_POOL_LAYERS, N_POOL_GROUPS, POOL_GROUP, POOL_GROUP), POOL_GROUP ** -0.5),
        "pool_scale": 0.5 + 0.1 * jax.random.normal(ks[13], (N_POOL_LAYERS, D_MODEL), f32),
        "w_conv_in": nrm(ks[14], (N_CONV_LAYERS, D_MODEL, 3 * D_CONV), D_MODEL ** -0.5),
        "w_conv": nrm(ks[15], (N_CONV_LAYERS, CONV_WIDTH, D_CONV), CONV_WIDTH ** -0.5),
        "w_conv_out": nrm(ks[16], (N_CONV_LAYERS, D_CONV, D_MODEL), D_CONV ** -0.5),
        "w_q": nrm(ks[17], (DEPTH, D_MODEL, D_MODEL), D_MODEL ** -0.5),
        "w_kv": nrm(ks[18], (DEPTH, D_MODEL, 2 * D_MODEL), D_MODEL ** -0.5),
        "w_o": nrm(ks[19], (DEPTH, D_MODEL, D_MODEL), D_MODEL ** -0.5),
        "w_up": nrm(ks[20], (DEPTH, D_MODEL, D_FF), D_MODEL ** -0.5),
        "w_down": nrm(ks[21], (DEPTH, D_FF, D_MODEL), D_FF ** -0.5),
    }


def reference(x_prompt, x_sample, state_pool, state_conv, cache_mem_k, cache_mem_v, mem_prompt,
              g_mix, g_attn, g_mem, g_ffn, g_final, w_pool, pool_scale,
              w_conv_in, w_conv, w_conv_out, w_q, w_kv, w_o, w_up, w_down):
    kvs = [mem_kv(mem_prompt, g_mem[i], w_kv[i]) for i in range(DEPTH)]
    mem_k_prompt = jnp.stack([kv[0] for kv in kvs])
    mem_v_prompt = jnp.stack([kv[1] for kv in kvs])
    b = x_prompt.shape[0]
    pool0 = jnp.zeros((N_POOL_LAYERS, b, POOL_STATE, D_MODEL), x_prompt.dtype)
    conv0 = jnp.zeros((N_CONV_LAYERS, b, CONV_STATE, D_CONV), x_prompt.dtype)
    y_prompt, new_pool_prompt, new_conv_prompt = trunk(
        x_prompt, pool0, conv0, mem_k_prompt, mem_v_prompt, 0,
        g_mix, g_attn, g_ffn, g_final, w_pool, pool_scale,
        w_conv_in, w_conv, w_conv_out, w_q, w_o, w_up, w_down)
    y_sample, new_pool_sample, new_conv_sample = trunk(
        x_sample, state_pool, state_conv, cache_mem_k, cache_mem_v, PAST_LEN,
        g_mix, g_attn, g_ffn, g_final, w_pool, pool_scale,
        w_conv_in, w_conv, w_conv_out, w_q, w_o, w_up, w_down)
    return (y_prompt, y_sample, new_pool_prompt, new_conv_prompt, mem_k_prompt, mem_v_prompt,
            new_pool_sample, new_conv_sample)
```

```python
import contextlib
from bisect import bisect_right
import numpy as np
import concourse.bass as bass
import concourse.mybir as mybir
from concourse.bass_utils import run_bass_kernel_spmd

F32 = mybir.dt.float32
BF16 = mybir.dt.bfloat16
AF = mybir.ActivationFunctionType
ALU = mybir.AluOpType
AX = mybir.AxisListType

PE, ACT, DVE, POOL, SP = "pe", "act", "dve", "pool", "sp"
NCORES = 8
D = 1024
NCH = 8
SEQ = 2048
NS = 16
NTOK = SEQ + NS
NMEM = 256
EPS = 1e-6
PSB = 1 << 24
MEM_BYTES = 212480

TILES = [(0, 416, 416, 0), (416, 416, 416, 0), (832, 416, 416, 0), (1248, 416, 416, 0), (1664, 400, 384, 16)]
NT = len(TILES)
TW = 416

R_MIX, R_ATTN, R_MEM, R_FFN, R_FINAL, R_PSCALE, R_WCONV = 0, 2, 4, 6, 8, 9, 10
NGV = 13
NBC = 1536 + 4 * 48
POOL_WINDOWS = (2, 4, 8, 16)


class V:
    __slots__ = ("ap", "ivs", "excl")

    def __init__(self, ap, ivs, excl=False):
        self.ap, self.ivs, self.excl = ap, ivs, excl


class Op:
    __slots__ = ("eng", "kind", "fn", "chan", "idx", "sig", "val", "cwaits", "dwaits")


class Seg:
    __slots__ = ("w", "rc", "rd")

    def __init__(self):
        self.w, self.rc, self.rd = None, {}, []

    def copy(self):
        s = Seg()
        s.w, s.rc, s.rd = self.w, dict(self.rc), list(self.rd)
        return s


class Tracker:
    def __init__(self):
        self.bounds = [0, 1 << 40]
        self.segs = [Seg()]

    def _split(self, x):
        i = bisect_right(self.bounds, x) - 1
        if self.bounds[i] == x:
            return i
        self.bounds.insert(i + 1, x)
        self.segs.insert(i + 1, self.segs[i].copy())
        return i + 1

    def access(self, lo, hi, op, write, deps):
        i = self._split(lo)
        j = self._split(hi)
        for s in self.segs[i:j]:
            if s.w is not None:
                deps.add(s.w)
            if write:
                deps.update(s.rc.values())
                deps.update(s.rd)
                s.rc, s.rd, s.w = {}, [], op
            else:
                if op.kind == "d":
                    s.rd.append(op)
                else:
                    s.rc[op.eng] = op


class Prog:
    def __init__(self):
        self.q = {e: [] for e in (PE, ACT, DVE, POOL, SP)}
        self.tr = Tracker()
        self.chan_count = {}
        self.last = {}

    def add(self, eng, kind, fn, reads, writes, chan=None, after=()):
        op = Op()
        op.eng, op.kind, op.fn, op.chan = eng, kind, fn, chan
        op.idx = len(self.q[eng])
        op.sig, op.val = False, 0
        deps = set()
        for v in reads:
            for lo, hi in v.ivs:
                self.tr.access(lo, hi, op, v.excl, deps)
        for v in writes:
            for lo, hi in v.ivs:
                self.tr.access(lo, hi, op, True, deps)
        deps.update(after)
        deps.discard(op)
        cw, dw = {}, {}
        for d in deps:
            if d.kind == "d":
                dw[d.chan] = self.chan_count[d.chan] * 16
            else:
                if d.eng == eng and kind == "c" and eng == PE:
                    continue
                if d.eng not in cw or cw[d.eng].idx < d.idx:
                    cw[d.eng] = d
        for d in cw.values():
            d.sig = True
        op.cwaits = list(cw.values())
        op.dwaits = list(dw.items())
        if kind == "d":
            self.chan_count[chan] = self.chan_count.get(chan, 0) + 1
        self.q[eng].append(op)
        self.last[eng] = op
        return op


def _prod(xs):
    r = 1
    for x in xs:
        r *= x
    return r


def build_program():
    nc = bass.Bass("TRN2", target_bir_lowering=False)
    P = Prog()

    def din(name, shape):
        return nc.dram_tensor(name, list(shape), F32, kind="ExternalInput").ap()

    def dout(name, shape):
        return nc.dram_tensor(name, list(shape), F32, kind="ExternalOutput").ap()

    x_d = din("x", [SEQ, D])
    xs_d = din("xs", [NS, D])
    spool_d = din("spool", [NS * 15, D])
    sconv_d = din("sconv", [NS * 2, D])
    ck_d = din("ck", [2, NS, NMEM, D])
    cv_d = din("cv", [2, NS, NMEM, D])
    mem_d = din("mem", [NMEM, D])
    gv_d = din("gv", [NGV, D])
    ident_d = din("ident", [128, 128])
    bands_d = din("bands", [128, NBC])
    wpool_d = din("w_pool", [4, 256, 256])
    wcin_d = din("w_conv_in", [D, 3 * D])
    wcout_d = din("w_conv_out", [D, D])
    wq_d = din("w_q", [2, D, D])
    wkv_d = din("w_kv", [2, D, 2 * D])
    wo_d = din("w_o", [2, D, D])
    wup_d = din("w_up", [2, D, 4 * D])
    wdn_d = din("w_down", [2, 4 * D, D])

    y_d = dout("y", [SEQ, D])
    ys_d = dout("ys", [NS, D])
    npp_d = dout("npp", [15, D])
    ncp_d = dout("ncp", [2, D])
    mk_d = dout("mk", [2, NMEM, D])
    mv_d = dout("mv", [2, NMEM, D])
    nps_d = dout("nps", [NS, 15, D])
    ncs_d = dout("ncs", [NS, 2, D])

    es = contextlib.ExitStack()
    MEM = es.enter_context(nc.sbuf_tensor("MEM", [128, MEM_BYTES // 2], BF16))
    PS = es.enter_context(nc.psum_tensor("PS", [128, 8, 512], F32))

    def DV(ap):
        return V(ap, [])

    class T:
        def __init__(self, off, dt, shape):
            assert off % 4 == 0
            self.off, self.dt, self.shape = off, dt, tuple(shape)
            self.esz = 4 if dt == F32 else 2
            n = _prod(shape)
            self.nbytes = n * self.esz
            assert off + self.nbytes <= MEM_BYTES, (off, self.nbytes)
            base = MEM[:, off // 2: (off + self.nbytes) // 2]
            if dt == F32:
                base = base.bitcast(F32)
            if len(shape) == 2:
                base = base.rearrange("p (a b) -> p a b", a=shape[0])
            elif len(shape) == 3:
                base = base.rearrange("p (a b c) -> p a b c", a=shape[0], b=shape[1])
            self.ap = base
            st, s = [], 1
            for d in reversed(shape):
                st.append(s)
                s *= d
            self.strides = tuple(reversed(st))

        def __call__(self, *idx, p=None):
            idx = list(idx) + [slice(None)] * (len(self.shape) - len(idx))
            rng = []
            for i, d in zip(idx, self.shape):
                if isinstance(i, int):
                    rng.append((i, i + 1))
                else:
                    a = 0 if i.start is None else i.start
                    b = d if i.stop is None else i.stop
                    assert 0 <= a < b <= d, (idx, self.shape)
                    rng.append((a, b))
            key = (slice(None) if p is None else slice(p[0], p[1]),) + tuple(idx)
            ap = self.ap[key]
            k = len(self.shape)
            while k > 1 and rng[k - 1] == (0, self.shape[k - 1]):
                k -= 1
            ivs = []

            def rec(dim, base):
                if dim == k - 1:
                    a, b = rng[dim]
                    lo = base + a * self.strides[dim]
                    hi = base + b * self.strides[dim]
                    ivs.append((self.off + lo * self.esz, self.off + hi * self.esz))
                    return
                for i in range(rng[dim][0], rng[dim][1]):
                    rec(dim + 1, base + i * self.strides[dim])
            rec(0, 0)
            return V(ap, ivs)

    def sl(a, b):
        return slice(a, b)

    def PSv(bank, a, b, m=128):
        return V(PS[0:m, bank, a:b], [(PSB + bank * 2048, PSB + (bank + 1) * 2048)], excl=True)

    def PSr(bank, pat, m=128, **kw):
        return V(PS[0:m, bank, :].rearrange(pat, **kw), [(PSB + bank * 2048, PSB + (bank + 1) * 2048)], excl=True)

    bank_ctr = [0]

    def nxt():
        b = bank_ctr[0] % 7
        bank_ctr[0] += 1
        return b

    top = [0]

    def palloc(dt, shape):
        t = T(top[0], dt, shape)
        top[0] += (t.nbytes + 63) // 64 * 64
        return t

    X = palloc(F32, [NCH, NTOK])
    U3 = palloc(BF16, [NCH, NTOK])
    WS = [palloc(BF16, [NCH, D]) for _ in range(3)]
    IDF = palloc(F32, [128])
    ONESB = palloc(BF16, [128])
    G = palloc(F32, [NCH, 16])
    INV = palloc(F32, [16])
    HALO = palloc(F32, [NCH, 2])
    PB = palloc(BF16, [NS, 2, 4])
    SS = palloc(F32, [NS, 2, 4])
    QS = palloc(BF16, [D])
    RDS = palloc(F32, [NS, 4])
    OTS = palloc(BF16, [NCH, NS])
    SCR0 = top[0]
    SCR_BYTES = MEM_BYTES - SCR0

    class Scratch:
        def __init__(self):
            self.top = SCR0

        def alloc(self, dt, shape):
            t = T(self.top, dt, shape)
            self.top += (t.nbytes + 63) // 64 * 64
            assert self.top <= MEM_BYTES, ("scratch overflow", self.top - SCR0, SCR_BYTES)
            return t

    def mm(out, lhsT, rhs, start, stop):
        P.add(PE, "c", lambda e, o=out.ap, l=lhsT.ap, r=rhs.ap: e.matmul(out=o, lhsT=l, rhs=r, start=start, stop=stop),
              [lhsT, rhs], [out])

    def mmx(out, lhsT, rhs, start, stop, is_transpose=False):
        P.add(PE, "c", lambda e, o=out.ap, l=lhsT.ap, r=rhs.ap: e.matmul(out=o, lhsT=l, rhs=r, start=start, stop=stop,
                                                                   is_transpose=is_transpose, skip_group_check=True),
              [lhsT, rhs], [out])

    def tp(out, in_, ident):
        P.add(PE, "c", lambda e, o=out.ap, i=in_.ap, d=ident.ap: e.transpose(out=o, in_=i, identity=d), [in_, ident], [out])

    def act(out, in_, func, scale=1.0, bias=0.0):
        rd = [in_]
        sc = scale
        if isinstance(scale, V):
            rd.append(scale)
            sc = scale.ap
        P.add(ACT, "c", lambda e, o=out.ap, i=in_.ap: e.activation(out=o, in_=i, func=func, bias=bias, scale=sc), rd, [out])

    def stt(out, in0, scalar, in1, op0, op1, eng=DVE):
        rd = [in0, in1]
        sc = scalar
        if isinstance(scalar, V):
            rd.append(scalar)
            sc = scalar.ap
        P.add(eng, "c", lambda e, o=out.ap, a=in0.ap, b=in1.ap: e.scalar_tensor_tensor(out=o, in0=a, scalar=sc, in1=b, op0=op0, op1=op1),
              rd, [out])

    def tt(out, in0, in1, op, eng=DVE):
        P.add(eng, "c", lambda e, o=out.ap, a=in0.ap, b=in1.ap: e.tensor_tensor(out=o, in0=a, in1=b, op=op), [in0, in1], [out])

    def tsc(out, in0, s1, op0, eng=DVE):
        rd = [in0]
        sc = s1
        if isinstance(s1, V):
            rd.append(s1)
            sc = s1.ap
        P.add(eng, "c", lambda e, o=out.ap, a=in0.ap: e.tensor_scalar(out=o, in0=a, scalar1=sc, scalar2=None, op0=op0), rd, [out])

    def cp(out, in_, eng=DVE):
        if eng == ACT:
            P.add(ACT, "c", lambda e, o=out.ap, i=in_.ap: e.copy(out=o, in_=i), [in_], [out])
        else:
            P.add(eng, "c", lambda e, o=out.ap, i=in_.ap: e.tensor_copy(out=o, in_=i), [in_], [out])

    def red(out, in_, eng=DVE):
        P.add(eng, "c", lambda e, o=out.ap, i=in_.ap: e.tensor_reduce(out=o, in_=i, axis=AX.X, op=ALU.add), [in_], [out])

    def memset(out, val, eng=DVE):
        P.add(eng, "c", lambda e, o=out.ap: e.memset(o, val), [], [out])

    def dma(q, out, in_, chan, after=()):
        P.add(q, "d", lambda e, o=out.ap, i=in_.ap: e.dma_start(out=o, in_=i), [in_], [out], chan=chan, after=after)

    evac_ctr = [0]

    def evac_eng():
        evac_ctr[0] += 1
        return ACT if evac_ctr[0] % 2 else DVE

    def wsrc_sq(ap2d):
        return ap2d.rearrange("(kc p) e -> p kc e", p=128)

    wlist = []
    for l in range(2):
        wlist.append((f"wk{l}", wsrc_sq(wkv_d[l][:, 0:D])))
        wlist.append((f"wv{l}", wsrc_sq(wkv_d[l][:, D:2 * D])))
        if l == 1:
            wlist.append(("cin_b", wsrc_sq(wcin_d[:, 0:D])))
            wlist.append(("cin_c", wsrc_sq(wcin_d[:, D:2 * D])))
            wlist.append(("cin_h", wsrc_sq(wcin_d[:, 2 * D:3 * D])))
            wlist.append(("cout", wsrc_sq(wcout_d)))
        wlist.append((f"wq{l}", wsrc_sq(wq_d[l])))
        wlist.append((f"wo{l}", wsrc_sq(wo_d[l])))
        for qq in range(4):
            wlist.append((f"up{l}_{qq}", wsrc_sq(wup_d[l][:, qq * D:(qq + 1) * D])))
            wlist.append((f"dn{l}_{qq}", wsrc_sq(wdn_d[l][qq * D:(qq + 1) * D, :])))
    widx = {n: i for i, (n, _) in enumerate(wlist)}
    wloaded = [0]

    def wload_next():
        i = wloaded[0]
        if i >= len(wlist):
            return
        wloaded[0] += 1
        dma(POOL, WS[i % 3](), DV(wlist[i][1]), f"w{i % 3}")

    def W(name):
        i = widx[name]
        assert i < wloaded[0], name
        return WS[i % 3]

    def wrelease(name):
        i = widx[name]
        while wloaded[0] <= i + 3 and wloaded[0] < len(wlist):
            wload_next()

    sc = Scratch()
    XS = [sc.alloc(F32, [D]) for _ in range(2)]
    GVS = sc.alloc(F32, [D])
    dma(SP, IDF(), DV(ident_d), "c")
    memset(ONESB(), 1.0)
    for j in range(15):
        memset(INV(sl(j, j + 1)), 1.0 / (j + 1))
    memset(HALO(), 0.0)

    def build_gains(GV2):
        bk = nxt()
        for c in range(NCH):
            tp(PSv(bk, c * 16, c * 16 + NGV), GV2(sl(c * 128, (c + 1) * 128), p=(0, NGV)), V(IDF.ap[0:NGV, 0:NGV], IDF().ivs))
        cp(G(sl(0, NCH), sl(0, NGV)), V(PS[:, bk, 0:128].rearrange("p (c r) -> p c r", c=NCH)[:, :, 0:NGV], PSv(bk, 0, 1).ivs, True))

    def mixer0_phase():
        s2 = Scratch()
        s2.top = sc.top
        XS3 = [XS[0], XS[1], GVS, s2.alloc(F32, [D])]
        NXS = 4
        SQJ = s2.alloc(F32, [D])
        UTM = [s2.alloc(BF16, [D]) for _ in range(3)]
        GBC = s2.alloc(F32, [D])
        PLB = [s2.alloc(BF16, [NCH, 128]) for _ in range(2)]
        BANDS = s2.alloc(BF16, [NBC])
        WP = s2.alloc(BF16, [2, 4, 256])
        UF = s2.alloc(F32, [D])
        UCF = s2.alloc(F32, [D])
        WPF = T(UF.off, F32, [2, 4, 256])
        assert UCF.off == UF.off + 4096
        STAT = s2.alloc(F32, [4, 4])
        STB = [s2.alloc(BF16, [D]) for _ in range(2)]
        UCB = s2.alloc(BF16, [D])
        PLS = s2.alloc(BF16, [NCH, NS])
        SCBv = T(SQJ.off, F32, [4, 256])

        def band(wi, kind):
            a = wi * 384 + kind * 128
            return BANDS(sl(a, a + 128))

        GV2 = T(STB[0].off, F32, [D])
        assert STB[1].off == STB[0].off + 2048
        dma(SP, GBC(), DV(gv_d[R_MIX:R_MIX + 1, :].broadcast_to([128, D])), "c2")
        for tb0 in range(2):
            dma(SP, XS3[tb0](), DV(x_d[tb0 * 128:(tb0 + 1) * 128, :]), f"xs{tb0}")
        dma(SP, SQJ(), DV(gv_d[R_PSCALE:R_PSCALE + 1, :].broadcast_to([128, D])), "c2")
        dma(POOL, BANDS(), DV(bands_d), "wp")
        dma(SP, GV2(p=(0, NGV)), DV(gv_d), "c")
        for cc in range(2):
            dma(SP, WPF(cc), DV(wpool_d[:, cc * 128:(cc + 1) * 128, :].rearrange("g p e -> p g e")), "c2")
        for cc in range(2):
            tt(WP(cc), WPF(cc), SCBv(), ALU.mult)

        def stat(i, col):
            return STAT(i % 4, sl(col, col + 1))

        def stA(tb):
            xs = XS3[tb % NXS]
            if tb >= 2:
                dma(SP, xs(), DV(x_d[tb * 128:(tb + 1) * 128, :]), f"xs{tb % NXS}")
            act(SQJ(), xs(), AF.Square)
            red(stat(tb, 0), SQJ())
            act(stat(tb, 1), stat(tb, 0), AF.Ln, scale=1.0 / D, bias=EPS)
            act(stat(tb, 2), stat(tb, 1), AF.Exp, scale=-0.5)
            stt(UTM[tb % 3](), xs(), stat(tb, 2), GBC(), ALU.mult, ALU.mult)
            if tb == 15:
                stt(UF(), xs(), stat(tb, 2), GBC(), ALU.mult, ALU.mult)
                dma(SP, DV(npp_d), UF(p=(113, 128)), "uf")

        def stB(tb):
            bks = [nxt(), nxt()]
            utm = UTM[tb % 3]
            for c in range(NCH):
                wi = c // 2
                out = PSv(bks[c // 4], (c % 4) * 128, (c % 4 + 1) * 128)
                if tb == 0:
                    mm(out, utm(sl(c * 128, (c + 1) * 128)), band(wi, 2), True, True)
                else:
                    mm(out, UTM[(tb - 1) % 3](sl(c * 128, (c + 1) * 128)), band(wi, 1), True, False)
                    mm(out, utm(sl(c * 128, (c + 1) * 128)), band(wi, 0), False, True)
            for half in range(2):
                cp(PLB[tb % 2](sl(half * 4, half * 4 + 4)), PSr(bks[half], "p (j t) -> p j t", j=4), eng=ACT)

        def stC(tb):
            bks = [nxt(), nxt()]
            pl = PLB[tb % 2]
            xs = XS3[tb % NXS]
            for c in range(NCH):
                mmx(PSv(bks[c // 4], (c % 4) * 128, (c % 4 + 1) * 128), xs(sl(c * 128, (c + 1) * 128)), IDF(),
                    c % 4 == 0, False, is_transpose=True)
            for g in range(4):
                for j in range(2):
                    ec = 2 * g + j
                    out = PSv(bks[ec // 4], (ec % 4) * 128, (ec % 4 + 1) * 128)
                    for cc in range(2):
                        mmx(out, WP(cc, g, sl(j * 128, (j + 1) * 128)), pl(2 * g + cc), False, cc == 1)
            for half in range(2):
                cp(X(sl(half * 4, half * 4 + 4), sl(tb * 128, (tb + 1) * 128)),
                   PSr(bks[half], "p (j t) -> p j t", j=4), eng=(ACT if half == 0 else DVE))

        pieces = [(wi_, kc) for wi_ in range(3) for kc in range(NCH)]

        def wpiece():
            if not pieces:
                return
            wi_, kc = pieces.pop(0)
            dma(POOL, WS[wi_ % 3](kc), DV(wlist[wi_][1][:, kc, :]), f"w{wi_ % 3}", after=((P.last[DVE],) if wi_ < 2 else ()))

        for step in range(16 + 2):
            if step < 16:
                stA(step)
                if step == 2:
                    build_gains(GV2)
                if step >= 1:
                    wpiece()
                    wpiece()
            if 0 <= step - 1 < 16:
                stB(step - 1)
            if 0 <= step - 2 < 16:
                stC(step - 2)

        while pieces:
            wpiece()
        wloaded[0] = 3
        for i in range(2):
            dma(SP, XS3[i](p=(0, 120)), DV(spool_d[i * 120:(i + 1) * 120, :]), f"xs{i}")
            cp(STB[i](p=(0, 120)), XS3[i](p=(0, 120)), eng=(ACT if i == 0 else DVE))
        dma(SP, DV(nps_d[:, 0:14, :]), DV(spool_d.rearrange("(b j) d -> b j d", j=15)[:, 1:15, :]), "d2d")
        xs = XS3[2]
        dma(SP, xs(p=(0, NS)), DV(xs_d), "xs2")
        bk = nxt()
        for c in range(NCH):
            tp(PSv(bk, c * 16, c * 16 + 16), xs(sl(c * 128, (c + 1) * 128), p=(0, NS)), V(IDF.ap[0:NS, 0:NS], IDF().ivs))
        cp(X(sl(0, NCH), sl(SEQ, NTOK)), V(PS[:, bk, 0:128].rearrange("p (c r) -> p c r", c=NCH), PSv(bk, 0, 1).ivs, True))
        P16 = (0, NS)
        act(SQJ(p=P16), xs(p=P16), AF.Square)
        red(STAT(0, sl(0, 1), p=P16), SQJ(p=P16))
        act(STAT(0, sl(1, 2), p=P16), STAT(0, sl(0, 1), p=P16), AF.Ln, scale=1.0 / D, bias=EPS)
        act(STAT(0, sl(2, 3), p=P16), STAT(0, sl(1, 2), p=P16), AF.Exp, scale=-0.5)
        stt(UCF(p=P16), xs(p=P16), STAT(0, sl(2, 3), p=P16), GBC(p=P16), ALU.mult, ALU.mult)
        cp(UCB(p=P16), UCF(p=P16), eng=ACT)
        dma(SP, DV(nps_d[:, 14, :]), UCF(p=P16), "ucf")
        bk = nxt()
        for c in range(NCH):
            wi = c // 2
            sb = 1536 + wi * 48
            out = PSv(bk, c * 16, c * 16 + 16)
            mm(out, STB[0](sl(c * 128, (c + 1) * 128), p=(0, 120)), BANDS(sl(sb, sb + 16), p=(0, 120)), True, False)
            mm(out, STB[1](sl(c * 128, (c + 1) * 128), p=(0, 120)), BANDS(sl(sb + 16, sb + 32), p=(0, 120)), False, False)
            mm(out, UCB(sl(c * 128, (c + 1) * 128), p=P16), BANDS(sl(sb + 32, sb + 48), p=P16), False, True)
        cp(PLS(), V(PS[:, bk, 0:128].rearrange("p (c b) -> p c b", c=NCH), PSv(bk, 0, 1).ivs, True), eng=ACT)
        bk = nxt()
        for g in range(4):
            for j in range(2):
                ec = 2 * g + j
                for cc in range(2):
                    mm(PSv(bk, ec * 16, ec * 16 + 16), WP(cc, g, sl(j * 128, (j + 1) * 128)), PLS(2 * g + cc), cc == 0, cc == 1)
        xv = X(sl(0, NCH), sl(SEQ, NTOK))
        tt(xv, V(PS[:, bk, 0:128].rearrange("p (c b) -> p c b", c=NCH), PSv(bk, 0, 1).ivs, True), xv, ALU.add)

    def norm_a(c0, n, SQ):
        act(SQ(sl(0, NCH), sl(0, n)), X(sl(0, NCH), sl(c0, c0 + n)), AF.Square)

    def norm_b(c0, n, grow, outv, SQ, RSTD, extra=None):
        bk = nxt()
        for c in range(NCH):
            mm(PSv(bk, 0, n), ONESB(), SQ(c, sl(0, n)), c == 0, c == NCH - 1)
        act(PSv(bk, 0, n), PSv(bk, 0, n), AF.Ln, scale=1.0 / D, bias=EPS)
        act(RSTD(sl(0, n)), PSv(bk, 0, n), AF.Exp, scale=-0.5)
        for c in range(NCH):
            stt(outv(c), X(c, sl(c0, c0 + n)), G(c, sl(grow, grow + 1)), RSTD(sl(0, n)), ALU.mult, ALU.mult)
        if extra is not None:
            extra(RSTD)

    def norm(c0, n, grow, outv, SQ, RSTD, extra=None):
        norm_a(c0, n, SQ)
        norm_b(c0, n, grow, outv, SQ, RSTD, extra)

    def kv_phase(l, KT, VV):
        o = U3.off
        ST = [T(o + i * 4096, F32, [D]) for i in range(3)]
        MEMT = T(o + 12288, F32, [NCH, NMEM])
        SQM = T(o + 20480, BF16, [NCH, NMEM])
        MEMN = T(o + 24576, BF16, [NCH, NMEM])
        RSM = T(o + 28672, F32, [NMEM])
        assert 28672 + 1024 <= U3.nbytes
        for nb in range(2):
            dma(SP, ST[nb](), DV(mem_d[nb * 128:(nb + 1) * 128, :]), f"st{nb}")
            for half in range(2):
                bk = nxt()
                for j in range(4):
                    c = half * 4 + j
                    tp(PSv(bk, j * 128, (j + 1) * 128), ST[nb](sl(c * 128, (c + 1) * 128)), IDF())
                cp(MEMT(sl(half * 4, half * 4 + 4), sl(nb * 128, (nb + 1) * 128)),
                   PSr(bk, "p (j t) -> p j t", j=4), eng=evac_eng())
        act(SQM(), MEMT(), AF.Square)
        bk = nxt()
        for c in range(NCH):
            mm(PSv(bk, 0, NMEM), ONESB(), SQM(c), c == 0, c == NCH - 1)
        act(PSv(bk, 0, NMEM), PSv(bk, 0, NMEM), AF.Ln, scale=1.0 / D, bias=EPS)
        act(RSM(), PSv(bk, 0, NMEM), AF.Exp, scale=-0.5)
        for c in range(NCH):
            stt(MEMN(c), MEMT(c), G(c, sl(R_MEM + l, R_MEM + l + 1)), RSM(), ALU.mult, ALU.mult)
        yield
        Wk, Wv = W(f"wk{l}"), W(f"wv{l}")
        for ep in range(4):
            bk = nxt()
            for j in range(2):
                ec = ep * 2 + j
                for kc in range(NCH):
                    mm(PSv(bk, j * 256, (j + 1) * 256), Wk(kc, sl(ec * 128, (ec + 1) * 128)), MEMN(kc), kc == 0, kc == NCH - 1)
            cp(KT(sl(ep * 2, ep * 2 + 2)), PSr(bk, "p (j n) -> p j n", j=2), eng=evac_eng())
            yield
        si = 0
        for which, Wt, od in ((0, Wk, mk_d), (1, Wv, mv_d)):
            for nb in range(2):
                st = ST[si % 3]
                for half in range(2):
                    bk = nxt()
                    for kc in range(NCH):
                        mm(PSv(bk, 0, 512), MEMN(kc, sl(nb * 128, (nb + 1) * 128)), Wt(kc, sl(half * 512, (half + 1) * 512)),
                           kc == 0, kc == NCH - 1)
                    cp(st(sl(half * 512, (half + 1) * 512)), PSv(bk, 0, 512), eng=ACT)
                    if which == 1:
                        cp(VV(nb, sl(half * 512, (half + 1) * 512)), st(sl(half * 512, (half + 1) * 512)), eng=DVE)
                    yield
                dma(SP, DV(od[l, nb * 128:(nb + 1) * 128, :]), st(), f"st{si % 3}")
                si += 1
        wrelease(f"wk{l}")
        wrelease(f"wv{l}")

    kv_scr_top = [SCR0]

    KT = palloc(BF16, [NCH, NMEM])
    VV = palloc(BF16, [2, D])
    SCRB = top[0]

    class Scratch2(Scratch):
        def __init__(self):
            self.top = SCRB
    kv_scr_top[0] = SCRB

    def pool_phase(kvgen):
        s2 = Scratch2()
        WP = s2.alloc(BF16, [2, 4, 256])
        SQ = s2.alloc(BF16, [NCH, TW])
        PL = s2.alloc(BF16, [NCH, TW])
        UBS = [s2.alloc(BF16, [NCH, 16 + TW]) for _ in range(2)]
        RSTDS = [s2.alloc(F32, [TW]) for _ in range(2)]
        U15 = s2.alloc(F32, [NCH, 16])
        FX = s2.alloc(F32, [16])
        SW = s2.alloc(F32, [NCH, NS])
        TA = [s2.alloc(F32, [16 + TW]) for _ in range(4)]
        TB = [s2.alloc(F32, [16 + TW]) for _ in range(4)]
        STG = [T(U3.off + i * 4096, F32, [D]) for i in range(2)]
        SPT = T(U3.off + 8192, F32, [NCH, NS, 16])
        for cc in range(2):
            dma(POOL, WP(cc), DV(wpool_d[:, cc * 128:(cc + 1) * 128, :].rearrange("g p e -> p g e")), "wp")
        memset(UBS[0](sl(0, NCH), sl(0, 16)), 0.0)
        dma(SP, STG[0](), DV(spool_d[0:128, :]), "stg0")
        dma(SP, STG[1](p=(0, 112)), DV(spool_d[128:240, :]), "stg1")
        for cpair in range(4):
            bk = nxt()
            for j in range(2):
                c = cpair * 2 + j
                tp(PSv(bk, j * 256, j * 256 + 128), STG[0](sl(c * 128, (c + 1) * 128)), IDF())
                tp(PSv(bk, j * 256 + 128, j * 256 + 240), STG[1](sl(c * 128, (c + 1) * 128), p=(0, 112)),
                   V(IDF.ap[0:112, 0:112], IDF().ivs))
            for j in range(2):
                c = cpair * 2 + j
                cp(SPT(c, sl(0, NS), sl(0, 15)),
                   V(PS[:, bk, j * 256:j * 256 + 240].rearrange("p (b j) -> p b j", b=NS), PSv(bk, 0, 1).ivs, True),
                   eng=evac_eng())
        dma(SP, DV(nps_d[:, 0:14, :]), DV(spool_d.rearrange("(b j) d -> b j d", j=15)[:, 1:15, :]), "d2d")

        def do_norm(ti):
            c0, n, npmt, ns = TILES[ti]
            last = ti == NT - 1
            UBt = UBS[ti % 2]

            def extra(RS, c0=c0, n=n, npmt=npmt):
                for c in range(NCH):
                    stt(SPT(c, sl(0, NS), 15), X(c, sl(c0 + npmt, c0 + n)), G(c, sl(R_MIX, R_MIX + 1)),
                        RS(sl(npmt, n)), ALU.mult, ALU.mult)
                    stt(U15(c, sl(0, 15)), X(c, sl(c0 + npmt - 15, c0 + npmt)), G(c, sl(R_MIX, R_MIX + 1)),
                        RS(sl(npmt - 15, npmt)), ALU.mult, ALU.mult)
            if ti > 0:
                pn = TILES[ti - 1][2]
                cp(UBt(sl(0, NCH), sl(0, 16)), UBS[(ti - 1) % 2](sl(0, NCH), sl(pn, pn + 16)))
            norm(c0, n, R_MIX + 0, lambda c, n=n: UBt(c, sl(16, 16 + n)), SQ, RSTDS[ti % 2], extra if last else None)

        do_norm(0)
        for ti in range(NT):
            c0, n, npmt, ns = TILES[ti]
            last = ti == NT - 1
            E = 16 + npmt
            UB = UBS[ti % 2]
            if not last:
                do_norm(ti + 1)
            for c in range(NCH):
                g = c // 2
                L = g + 1
                w = 1 << L
                we = POOL if c >= 4 else DVE
                pi = 0 if c < 4 else 1 + (c - 4) % 3
                ta, tb = TA[pi], TB[pi]
                tt(ta(sl(2, E)), UB(c, sl(2, E)), UB(c, sl(1, E - 1)), ALU.add, eng=we)
                if L >= 2:
                    tt(tb(sl(4, E)), ta(sl(4, E)), ta(sl(2, E - 2)), ALU.add, eng=we)
                if L >= 3:
                    tt(ta(sl(8, E)), tb(sl(8, E)), tb(sl(4, E - 4)), ALU.add, eng=we)
                if L >= 4:
                    tt(tb(sl(16, E)), ta(sl(16, E)), ta(sl(8, E - 8)), ALU.add, eng=we)
                al = ta if L in (1, 3) else tb
                stt(PL(c, sl(0, npmt)), al(sl(16, E)), 1.0 / w, UB(c, sl(16, E)), ALU.mult, ALU.subtract)
                if ti == 0:
                    tt(FX(sl(0, w - 1)), al(sl(16, 16 + w - 1)), INV(sl(0, w - 1)), ALU.mult)
                    tt(PL(c, sl(0, w - 1)), FX(sl(0, w - 1)), UB(c, sl(16, 16 + w - 1)), ALU.subtract)
            if last:
                for g in range(4):
                    w = 1 << (g + 1)
                    red(SW(sl(2 * g, 2 * g + 2), sl(0, NS)), SPT(sl(2 * g, 2 * g + 2), sl(0, NS), sl(16 - w, 16)))
                    stt(PL(sl(2 * g, 2 * g + 2), sl(npmt, n)), SW(sl(2 * g, 2 * g + 2), sl(0, NS)), 1.0 / w,
                        SPT(sl(2 * g, 2 * g + 2), sl(0, NS), 15), ALU.mult, ALU.subtract)
            for g in range(4):
                for j in range(2):
                    ec = 2 * g + j
                    bk = nxt()
                    for cc in range(2):
                        mm(PSv(bk, 0, n), WP(cc, g, sl(j * 128, (j + 1) * 128)), PL(2 * g + cc, sl(0, n)), cc == 0, cc == 1)
                    stt(X(ec, sl(c0, c0 + n)), PSv(bk, 0, n), G(ec, sl(R_PSCALE, R_PSCALE + 1)), X(ec, sl(c0, c0 + n)),
                        ALU.mult, ALU.add)
            for _ in range(3):
                next(kvgen, None)
        for _ in kvgen:
            pass
        bks = [nxt(), nxt()]
        for c in range(NCH):
            tp(PSv(bks[c // 4], (c % 4) * 128, (c % 4 + 1) * 128, m=15), U15(c, sl(0, 15)), IDF())
        for hf in range(2):
            cp(STG[0](sl(hf * 512, (hf + 1) * 512), p=(0, 15)), PSv(bks[hf], 0, 512, m=15), eng=evac_eng())
        dma(SP, DV(npp_d), STG[0](p=(0, 15)), "stg0")
        bks = [nxt(), nxt()]
        for c in range(NCH):
            tp(PSv(bks[c // 4], (c % 4) * 128, (c % 4 + 1) * 128, m=NS), SPT(c, sl(0, NS), 15), IDF())
        for hf in range(2):
            cp(STG[1](sl(hf * 512, (hf + 1) * 512), p=(0, NS)), PSv(bks[hf], 0, 512, m=NS), eng=evac_eng())
        dma(SP, DV(nps_d[:, 14, :]), STG[1](p=(0, NS)), "stg1")

    def conv_phase():
        s2 = Scratch2()
        SQ = s2.alloc(BF16, [NCH, TW])
        RSTD = [s2.alloc(F32, [TW]) for _ in range(2)]
        UT = [s2.alloc(BF16, [NCH, TW]) for _ in range(2)]
        GC = [s2.alloc(F32, [TW]) for _ in range(2)]
        ACC = [s2.alloc(F32, [TW]) for _ in range(2)]
        CH = [s2.alloc(F32, [2 + TW]) for _ in range(2)]
        SCT = s2.alloc(F32, [NCH, NS, 2])
        NCt = s2.alloc(F32, [NCH, 2])
        CHS = s2.alloc(F32, [NCH, NS])
        STG = [s2.alloc(F32, [D]) for _ in range(2)]
        Wb, Wc, Wh = W("cin_b"), W("cin_c"), W("cin_h")
        dma(SP, STG[0](p=(0, 32)), DV(sconv_d), "stg0")
        bk = nxt()
        for c in range(NCH):
            tp(PSv(bk, c * 32, (c + 1) * 32), STG[0](sl(c * 128, (c + 1) * 128), p=(0, 32)), V(IDF.ap[0:32, 0:32], IDF().ivs))
        cp(SCT(), V(PS[:, bk, 0:256].rearrange("p (c b j) -> p c b j", c=NCH, b=NS), PSv(bk, 0, 1).ivs, True))
        dma(SP, DV(ncs_d[:, 0, :]), DV(sconv_d.rearrange("(b j) d -> b j d", j=2)[:, 1, :]), "d2d")

        def nA(ti):
            c0, n, npmt, ns = TILES[ti]
            norm_a(c0, n, SQ)

        def nB(ti):
            c0, n, npmt, ns = TILES[ti]
            norm_b(c0, n, R_MIX + 1, lambda c, n=n, ti=ti: UT[ti % 2](c, sl(0, n)), SQ, RSTD[ti % 2])

        def stB(ti, mid=None):
            c0, n, npmt, ns = TILES[ti]
            last = ti == NT - 1
            ut = UT[ti % 2]
            for c in range(NCH):
                if c == 4 and mid is not None:
                    mid()
                k = c % 2
                bb, bc_, bh = nxt(), nxt(), nxt()
                for bkx, Wt in ((bc_, Wc), (bh, Wh), (bb, Wb)):
                    for kc in range(NCH):
                        mm(PSv(bkx, 0, n), Wt(kc, sl(c * 128, (c + 1) * 128)), ut(kc, sl(0, n)), kc == 0, kc == NCH - 1)
                cp(GC[k](sl(0, n)), PSv(bc_, 0, n), eng=ACT)
                cp(CH[k](sl(0, 2)), HALO(c))
                tt(CH[k](sl(2, 2 + n)), PSv(bh, 0, n), GC[k](sl(0, n)), ALU.mult)
                act(ACC[k](sl(0, n)), CH[k](sl(2, 2 + n)), AF.Copy, scale=G(c, sl(R_WCONV + 2, R_WCONV + 3)))
                stt(ACC[k](sl(0, npmt)), CH[k](sl(1, 1 + npmt)), G(c, sl(R_WCONV + 1, R_WCONV + 2)), ACC[k](sl(0, npmt)),
                    ALU.mult, ALU.add)
                stt(ACC[k](sl(0, npmt)), CH[k](sl(0, npmt)), G(c, sl(R_WCONV + 0, R_WCONV + 1)), ACC[k](sl(0, npmt)),
                    ALU.mult, ALU.add)
                if last:
                    stt(ACC[k](sl(npmt, n)), SCT(c, sl(0, NS), 1), G(c, sl(R_WCONV + 1, R_WCONV + 2)), ACC[k](sl(npmt, n)),
                        ALU.mult, ALU.add)
                    stt(ACC[k](sl(npmt, n)), SCT(c, sl(0, NS), 0), G(c, sl(R_WCONV + 0, R_WCONV + 1)), ACC[k](sl(npmt, n)),
                        ALU.mult, ALU.add)
                    cp(NCt(c), CH[k](sl(npmt, npmt + 2)))
                    cp(CHS(c), CH[k](sl(2 + npmt, 2 + n)))
                else:
                    cp(HALO(c), CH[k](sl(npmt, npmt + 2)))
                tt(U3(c, sl(c0, c0 + n)), PSv(bb, 0, n), ACC[k](sl(0, n)), ALU.mult)

        nA(0)
        nB(0)
        nA(1)
        for ti in range(NT):
            if ti + 1 < NT:
                nB(ti + 1)
            stB(ti, (lambda ti=ti: nA(ti + 2)) if ti + 2 < NT else None)
        wrelease("cin_b")
        wrelease("cin_c")
        wrelease("cin_h")
        bks = [nxt(), nxt()]
        for c in range(NCH):
            tp(PSv(bks[c // 4], (c % 4) * 128, (c % 4 + 1) * 128, m=2), NCt(c), IDF())
        for hf in range(2):
            cp(STG[0](sl(hf * 512, (hf + 1) * 512), p=(0, 2)), PSv(bks[hf], 0, 512, m=2), eng=evac_eng())
        dma(SP, DV(ncp_d), STG[0](p=(0, 2)), "stg0")
        bks = [nxt(), nxt()]
        for c in range(NCH):
            tp(PSv(bks[c // 4], (c % 4) * 128, (c % 4 + 1) * 128, m=NS), CHS(c), IDF())
        for hf in range(2):
            cp(STG[1](sl(hf * 512, (hf + 1) * 512), p=(0, NS)), PSv(bks[hf], 0, 512, m=NS), eng=evac_eng())
        dma(SP, DV(ncs_d[:, 1, :]), STG[1](p=(0, NS)), "stg1")
        Wco = W("cout")
        for ti in range(NT):
            c0, n, npmt, ns = TILES[ti]
            for ec in range(NCH):
                bk = nxt()
                for kc in range(NCH):
                    mm(PSv(bk, 0, n), Wco(kc, sl(ec * 128, (ec + 1) * 128)), U3(kc, sl(c0, c0 + n)), kc == 0, kc == NCH - 1)
                tt(X(ec, sl(c0, c0 + n)), PSv(bk, 0, n), X(ec, sl(c0, c0 + n)), ALU.add)
        wrelease("cout")

    def attn_phase(l):
        s2 = Scratch2()
        SQ = s2.alloc(BF16, [NCH, TW])
        QT = s2.alloc(BF16, [NCH, TW])
        BT = [s2.alloc(BF16, [NCH, TW]) for _ in range(2)]
        PT = [s2.alloc(BF16, [2, TW]) for _ in range(2)]
        RSTD = [s2.alloc(F32, [TW]) for _ in range(2)]
        RD = [s2.alloc(F32, [TW]) for _ in range(2)]
        KS = [T(U3.off + i * 4096, F32, [D]) for i in range(4)]
        VS = [T(U3.off + 16384 + i * 2048, BF16, [2, 512]) for i in range(4)]
        PRf = [T(U3.off + 24576 + i * 4096, F32, [D]) for i in range(2)]
        PR = [T(U3.off + 24576 + i * 4096, F32, [4, 256]) for i in range(2)]
        assert 24576 + 8192 <= U3.nbytes
        QBC = [s2.alloc(BF16, [D]) for _ in range(2)]
        SEL = s2.alloc(BF16, [NS, 128])
        RDSf = T(RDS.off, F32, [NS * 4])
        Wq, Wo = W(f"wq{l}"), W(f"wo{l}")
        cp(SEL(p=(0, NS)), V(IDF.ap[0:NS, 0:NS].unsqueeze(2).broadcast_to([NS, NS, 128]), IDF().ivs))
        seq = [4, 0, 1, 2, 3]

        def N2(k):
            N2a(k)
            N2b(k)

        def N2a(k):
            c0, n, npmt, ns = TILES[seq[k]]
            norm_a(c0, n, SQ)

        def N2b(k):
            c0, n, npmt, ns = TILES[seq[k]]
            norm_b(c0, n, R_ATTN + l, lambda c, n=n, k=k: BT[k % 2](c, sl(0, n)), SQ, RSTD[k % 2])

        def Q(k, mid=None):
            ti = seq[k]
            c0, n, npmt, ns = TILES[ti]
            ut = BT[k % 2]
            for ec in range(NCH):
                bk = nxt()
                for kc in range(NCH):
                    mm(PSv(bk, 0, n), Wq(kc, sl(ec * 128, (ec + 1) * 128)), ut(kc, sl(0, n)), kc == 0, kc == NCH - 1)
                cp(QT(ec, sl(0, n)), PSv(bk, 0, n), eng=ACT)
                if ec == 3 and mid is not None:
                    mid()
            if ns:
                for hf in range(2):
                    bk = nxt()
                    for kc in range(NCH):
                        mm(PSv(bk, 0, 512, m=NS), ut(kc, sl(npmt, n)), Wq(kc, sl(hf * 512, (hf + 1) * 512)), kc == 0, kc == NCH - 1)
                    cp(QS(sl(hf * 512, (hf + 1) * 512), p=(0, NS)), PSv(bk, 0, 512, m=NS), eng=ACT)

        def ATT(k, slot=None):
            ti = seq[k]
            c0, n, npmt, ns = TILES[ti]
            ot = BT[k % 2]

            def S(h):
                pt = PT[h % 2]
                for ncx in range(2):
                    bk = nxt()
                    for j in range(2):
                        mm(PSv(bk, 0, npmt), KT(2 * h + j, sl(ncx * 128, (ncx + 1) * 128)), QT(2 * h + j, sl(0, npmt)), j == 0, j == 1)
                    act(pt(ncx, sl(0, npmt)), PSv(bk, 0, npmt), AF.Exp, scale=1.0 / 16.0)

            def PVh(h):
                pt = PT[h % 2]
                bo = [nxt(), nxt()]
                bd = nxt()
                for ncx in range(2):
                    mm(PSv(bd, 0, npmt), ONESB(), pt(ncx, sl(0, npmt)), ncx == 0, ncx == 1)
                for j in range(2):
                    for ncx in range(2):
                        mm(PSv(bo[j], 0, npmt), VV(ncx, sl((2 * h + j) * 128, (2 * h + j + 1) * 128)), pt(ncx, sl(0, npmt)),
                           ncx == 0, ncx == 1)
                act(PSv(bd, 0, npmt), PSv(bd, 0, npmt), AF.Ln)
                act(RD[h % 2](sl(0, npmt)), PSv(bd, 0, npmt), AF.Exp, scale=-1.0)
                for j in range(2):
                    tt(ot(2 * h + j, sl(0, npmt)), PSv(bo[j], 0, npmt), RD[h % 2](sl(0, npmt)), ALU.mult)

            S(0)
            for h in range(4):
                if h + 1 < 4:
                    S(h + 1)
                PVh(h)
                if slot is not None:
                    slot(h)

        def O(k, mid=None):
            ti = seq[k]
            c0, n, npmt, ns = TILES[ti]
            ot = BT[k % 2]
            for ec in range(NCH):
                bk = nxt()
                for kc in range(NCH):
                    mm(PSv(bk, 0, npmt), Wo(kc, sl(ec * 128, (ec + 1) * 128)), ot(kc, sl(0, npmt)), kc == 0, kc == NCH - 1)
                tt(X(ec, sl(c0, c0 + npmt)), PSv(bk, 0, npmt), X(ec, sl(c0, c0 + npmt)), ALU.add)
                if ec == 3 and mid is not None:
                    mid()

        def SD(b):
            for ncx in range(2):
                i = (2 * b + ncx) % 4
                dma(SP, KS[i](), DV(ck_d[l, b, ncx * 128:(ncx + 1) * 128, :]), f"ks{i}")
            for hf in range(2):
                i = (2 * b + hf) % 4
                dma(POOL, VS[i](), DV(cv_d[l, b].rearrange("(nc p) e -> p nc e", p=128)[:, :, hf * 512:(hf + 1) * 512]), f"vs{i}")

        def SA(b):
            qb = QBC[b % 2]
            for h2 in range(2):
                bk = nxt()
                mm(PSv(bk, 0, 512), SEL(b, p=(0, NS)), QS(sl(h2 * 512, (h2 + 1) * 512), p=(0, NS)), True, True)
                cp(qb(sl(h2 * 512, (h2 + 1) * 512)), PSv(bk, 0, 512), eng=ACT)
            for ncx in range(2):
                i = (2 * b + ncx) % 4
                tt(PRf[ncx](), KS[i](), qb(), ALU.mult)
                red(SS(b, ncx), PR[ncx]())

        def SE(b):
            act(PB(b), SS(b), AF.Exp, scale=1.0 / 16.0)

        def SB(b):
            for ncx in range(2):
                mm(PSv(7, 128 + b * 4, 128 + b * 4 + 4), ONESB(), PB(b, ncx), ncx == 0, ncx == 1)
            for hf in range(2):
                i = (2 * b + hf) % 4
                for cc in range(4):
                    c = hf * 4 + cc
                    h = c // 2
                    for ncx in range(2):
                        mm(PSv(7, c * 16 + b, c * 16 + b + 1), VS[i](ncx, sl(cc * 128, (cc + 1) * 128)), PB(b, ncx, sl(h, h + 1)),
                           ncx == 0, ncx == 1)

        N2(0)
        Q(0)
        SD(0)
        sctr = [0]

        def sample_slot():
            b = sctr[0]
            if 0 < b <= NS:
                SE(b - 1)
                SB(b - 1)
            if b + 1 < NS:
                SD(b + 1)
            if b < NS:
                SA(b)
            sctr[0] += 1

        def make_slot(k):
            def slot(h):
                if h == 0 and k + 1 < 5:
                    N2a(k + 1)
                if h == 1 and k + 1 < 5:
                    N2b(k + 1)
                if h == 1 or h == 3:
                    sample_slot()
            return slot

        for k in range(5):
            ATT(k, make_slot(k))
            O(k, sample_slot)
            if k + 1 < 5:
                Q(k + 1, sample_slot)
        while sctr[0] <= NS:
            sample_slot()
        assert sctr[0] > NS
        wrelease(f"wq{l}")

        def finish():
            act(PSv(7, 128, 192), PSv(7, 128, 192), AF.Ln)
            act(RDSf(), PSv(7, 128, 192), AF.Exp, scale=-1.0)
            for c in range(NCH):
                tt(OTS(c), PSv(7, c * 16, c * 16 + 16), RDS(sl(0, NS), c // 2), ALU.mult)
            for ec in range(NCH):
                bk = nxt()
                for kc in range(NCH):
                    mm(PSv(bk, 0, NS), Wo(kc, sl(ec * 128, (ec + 1) * 128)), OTS(kc), kc == 0, kc == NCH - 1)
                tt(X(ec, sl(SEQ, NTOK)), PSv(bk, 0, NS), X(ec, sl(SEQ, NTOK)), ALU.add)
            wrelease(f"wo{l}")
        return finish

    def mlp_phase(l, finish_attn):
        fin = l == 1
        s2 = Scratch() if fin else Scratch2()
        SQ = s2.alloc(BF16, [NCH, TW])
        RSTD = [s2.alloc(F32, [TW]) for _ in range(2)]
        H = [s2.alloc(BF16, [NCH, TW]) for _ in range(2)]
        R = [s2.alloc(F32, [TW]) for _ in range(4)]
        if fin:
            STG = [s2.alloc(F32, [D]) for _ in range(2)]
            GB = s2.alloc(F32, [D])
            SQT = s2.alloc(F32, [D])
            SSs = s2.alloc(F32, [4])
            dma(SP, GB(), DV(gv_d[R_FINAL:R_FINAL + 1, :].broadcast_to([128, D])), "gb")
        rc = [0]
        stg_ctr = [0]

        def N3(ti):
            c0, n, npmt, ns = TILES[ti]
            if ti == NT - 1:
                finish_attn()
            norm(c0, n, R_FFN + l, lambda c, c0=c0, n=n: U3(c, sl(c0, c0 + n)), SQ, RSTD[ti % 2])

        def stA(q, ti, j):
            c0, n, npmt, ns = TILES[ti]
            Wu = W(f"up{l}_{q}")
            h = H[j % 2]
            for fc in range(NCH):
                bk = nxt()
                for kc in range(NCH):
                    mm(PSv(bk, 0, n), Wu(kc, sl(fc * 128, (fc + 1) * 128)), U3(kc, sl(c0, c0 + n)), kc == 0, kc == NCH - 1)
                r = R[rc[0] % 4]
                rc[0] += 1
                act(r(sl(0, n)), PSv(bk, 0, n), AF.Relu)
                tt(h(fc, sl(0, n)), r(sl(0, n)), r(sl(0, n)), ALU.mult)
            if ti == NT - 1:
                wrelease(f"up{l}_{q}")

        def stB(q, ti, j):
            c0, n, npmt, ns = TILES[ti]
            Wd = W(f"dn{l}_{q}")
            h = H[j % 2]
            for ec in range(NCH):
                bk = nxt()
                for fc in range(NCH):
                    mm(PSv(bk, 0, n), Wd(fc, sl(ec * 128, (ec + 1) * 128)), h(fc, sl(0, n)), fc == 0, fc == NCH - 1)
                tt(X(ec, sl(c0, c0 + n)), PSv(bk, 0, n), X(ec, sl(c0, c0 + n)), ALU.add)
            if ti == NT - 1:
                wrelease(f"dn{l}_{q}")

        def final_tile(ti):
            c0, n, npmt, ns = TILES[ti]
            blocks = [(a, min(npmt, a + 128), False) for a in range(0, npmt, 128)]
            if ns:
                blocks.append((npmt, n, True))
            for a, b, is_s in blocks:
                m = b - a
                bks = [nxt(), nxt()]
                for c in range(NCH):
                    tp(PSv(bks[c // 4], (c % 4) * 128, (c % 4 + 1) * 128, m=m), X(c, sl(c0 + a, c0 + b)), IDF())
                k = stg_ctr[0] % 2
                stg_ctr[0] += 1
                st = STG[k]
                for hf in range(2):
                    act(SQT(sl(hf * 512, (hf + 1) * 512), p=(0, m)), PSv(bks[hf], 0, 512, m=m), AF.Square)
                red(SSs(sl(0, 1), p=(0, m)), SQT(p=(0, m)))
                act(SSs(sl(1, 2), p=(0, m)), SSs(sl(0, 1), p=(0, m)), AF.Ln, scale=1.0 / D, bias=EPS)
                act(SSs(sl(2, 3), p=(0, m)), SSs(sl(1, 2), p=(0, m)), AF.Exp, scale=-0.5)
                for hf in range(2):
                    stt(st(sl(hf * 512, (hf + 1) * 512), p=(0, m)), PSv(bks[hf], 0, 512, m=m), SSs(sl(2, 3), p=(0, m)),
                        GB(sl(hf * 512, (hf + 1) * 512), p=(0, m)), ALU.mult, ALU.mult)
                if is_s:
                    dma(SP, DV(ys_d), st(p=(0, m)), f"stg{k}")
                else:
                    dma(SP, DV(y_d[c0 + a:c0 + b, :]), st(p=(0, m)), f"stg{k}")

        items = [(q, ti) for q in range(4) for ti in range(NT)]
        N3(0)
        N3(1)
        stA(0, 0, 0)
        pend = []
        for j, (q, ti) in enumerate(items):
            if q == 0 and ti + 2 < NT:
                N3(ti + 2)
            if j + 1 < len(items):
                stA(items[j + 1][0], items[j + 1][1], j + 1)
            if fin and pend:
                final_tile(pend.pop(0))
            stB(q, ti, j)
            if fin and q == 3:
                pend.append(ti)
                if ti == NT - 1:
                    while pend:
                        final_tile(pend.pop(0))
        while fin and pend:
            final_tile(pend.pop(0))

    mixer0_phase()
    for _ in kv_phase(0, KT, VV):
        pass
    fin = attn_phase(0)
    mlp_phase(0, fin)
    for _ in kv_phase(1, KT, VV):
        pass
    conv_phase()
    fin = attn_phase(1)
    mlp_phase(1, fin)

    sems = {e: es.enter_context(nc.semaphore(f"s_{e}")) for e in (PE, ACT, DVE, POOL)}
    chans = {c: es.enter_context(nc.semaphore(f"c_{c}")) for c in P.chan_count}
    for e in (PE, ACT, DVE, POOL, SP):
        cnt = 0
        for op in P.q[e]:
            if op.kind == "c" and op.sig:
                cnt += 1
            op.val = cnt

    def emit(name, e):
        known = {}
        for op in P.q[name]:
            for d in op.cwaits:
                key = ("e", d.eng)
                if known.get(key, 0) < d.val:
                    e.wait_ge(sems[d.eng], d.val)
                    known[key] = d.val
            for ch, val in op.dwaits:
                key = ("c", ch)
                if known.get(key, 0) < val:
                    e.wait_ge(chans[ch], val)
                    known[key] = val
            ins = op.fn(e)
            if op.kind == "d":
                ins.then_inc(chans[op.chan], 16)
            elif op.sig:
                ins.then_inc(sems[name], 1)
        if name == SP:
            for ch, cnt in P.chan_count.items():
                if known.get(("c", ch), 0) < cnt * 16:
                    e.wait_ge(chans[ch], cnt * 16)

    with nc.Block() as block:
        @block.tensor
        def _(e):
            emit(PE, e)

        @block.scalar
        def _(e):
            emit(ACT, e)

        @block.vector
        def _(e):
            emit(DVE, e)

        @block.gpsimd
        def _(e):
            emit(POOL, e)

        @block.sync
        def _(e):
            emit(SP, e)
    es.close()
    stats = {k: len(v) for k, v in P.q.items()}
    return nc, stats


_CACHE = {}


def _make_bands():
    out = np.zeros((128, NBC), np.float32)
    tp_ = np.arange(128)[:, None]
    t = np.arange(128)[None, :]
    for wi, w in enumerate(POOL_WINDOWS):
        inwin = (t - tp_ >= 0) & (t - tp_ < w)
        eye = (t == tp_).astype(np.float32)
        out[:, wi * 384:wi * 384 + 128] = inwin / float(w) - eye
        out[:, wi * 384 + 128:wi * 384 + 256] = ((128 + t - tp_) < w) / float(w)
        cnt = np.minimum(t + 1, w).astype(np.float32)
        out[:, wi * 384 + 256:wi * 384 + 384] = inwin / cnt - eye
        sb = 1536 + wi * 48
        for bl in range(8):
            for j in range(15):
                if j >= 16 - w:
                    out[bl * 15 + j, sb + bl] = 1.0 / w
                    out[bl * 15 + j, sb + 16 + 8 + bl] = 1.0 / w
        for r in range(16):
            out[r, sb + 32 + r] = 1.0 / w - 1.0
    return out


def kernel(x_prompt, x_sample, state_pool, state_conv, cache_mem_k, cache_mem_v, mem_prompt,
           g_mix, g_attn, g_mem, g_ffn, g_final, w_pool, pool_scale,
           w_conv_in, w_conv, w_conv_out, w_q, w_kv, w_o, w_up, w_down):
    f = lambda a: np.ascontiguousarray(np.asarray(a, dtype=np.float32))
    if "nc" not in _CACHE:
        _CACHE["nc"] = build_program()[0]
    nc = _CACHE["nc"]
    x_prompt, x_sample, state_pool, state_conv = f(x_prompt), f(x_sample), f(state_pool), f(state_conv)
    cache_mem_k, cache_mem_v, mem_prompt = f(cache_mem_k), f(cache_mem_v), f(mem_prompt)
    gv = f(np.concatenate([f(g_mix), f(g_attn), f(g_mem), f(g_ffn), f(g_final)[None, :], f(pool_scale), f(w_conv)[0]], axis=0))
    assert gv.shape == (NGV, D)
    shared = {
        "gv": gv, "ident": np.eye(128, dtype=np.float32), "bands": _make_bands(),
        "w_pool": f(w_pool)[0], "w_conv_in": f(w_conv_in)[0], "w_conv_out": f(w_conv_out)[0],
        "w_q": f(w_q), "w_kv": f(w_kv), "w_o": f(w_o), "w_up": f(w_up), "w_down": f(w_down),
    }
    in_maps = []
    for b in range(NCORES):
        s0, s1 = b * NS, (b + 1) * NS
        m = dict(shared)
        m["x"] = x_prompt[b]
        m["xs"] = f(x_sample[s0:s1, 0, :])
        m["spool"] = f(state_pool[0, s0:s1].reshape(NS * 15, D))
        m["sconv"] = f(state_conv[0, s0:s1].reshape(NS * 2, D))
        m["ck"] = f(cache_mem_k[:, s0:s1].reshape(2, NS, NMEM, D))
        m["cv"] = f(cache_mem_v[:, s0:s1].reshape(2, NS, NMEM, D))
        m["mem"] = mem_prompt[b]
        in_maps.append(m)
    res = run_bass_kernel_spmd(nc, in_maps, core_ids=list(range(NCORES)))
    rs = res.results
    g = lambda k: [np.asarray(r[k], dtype=np.float32) for r in rs]
    y_prompt = np.stack(g("y"), 0)
    y_sample = np.concatenate(g("ys"), 0).reshape(NCORES * NS, 1, D)
    new_pool_prompt = np.stack(g("npp"), 0)[None]
    new_conv_prompt = np.stack(g("ncp"), 0)[None]
    mem_k_prompt = np.stack(g("mk"), 1).reshape(2, NCORES, NMEM, 4, 256)
    mem_v_prompt = np.stack(g("mv"), 1).reshape(2, NCORES, NMEM, 4, 256)
    new_pool_sample = np.concatenate(g("nps"), 0)[None]
    new_conv_sample = np.concatenate(g("ncs"), 0)[None]
    return (y_prompt, y_sample, new_pool_prompt, new_conv_prompt, mem_k_prompt, mem_v_prompt,
            new_pool_sample, new_conv_sample)
```

```python
import contextlib
from bisect import bisect_right
import numpy as np
import concourse.bass as bass
import concourse.mybir as mybir
from concourse.bass_utils import run_bass_kernel_spmd

F32 = mybir.dt.float32
BF16 = mybir.dt.bfloat16
AF = mybir.ActivationFunctionType
ALU = mybir.AluOpType
AX = mybir.AxisListType

PE, ACT, DVE, POOL, SP = "pe", "act", "dve", "pool", "sp"
NCORES = 8
D = 1024
NCH = 8
SEQ = 2048
NS = 16
NTOK = SEQ + NS
NMEM = 256
EPS = 1e-6
PSB = 1 << 24
MEM_BYTES = 212480

TILES = [(0, 416, 416, 0), (416, 416, 416, 0), (832, 416, 416, 0), (1248, 416, 416, 0), (1664, 400, 384, 16)]
NT = len(TILES)
TW = 416

R_MIX, R_ATTN, R_MEM, R_FFN, R_FINAL, R_PSCALE, R_WCONV = 0, 2, 4, 6, 8, 9, 10
NGV = 13
NBC = 1536 + 4 * 48
POOL_WINDOWS = (2, 4, 8, 16)


class V:
    __slots__ = ("ap", "ivs", "excl")

    def __init__(self, ap, ivs, excl=False):
        self.ap, self.ivs, self.excl = ap, ivs, excl


class Op:
    __slots__ = ("eng", "kind", "fn", "chan", "idx", "sig", "val", "cwaits", "dwaits")


class Seg:
    __slots__ = ("w", "rc", "rd")

    def __init__(self):
        self.w, self.rc, self.rd = None, {}, []

    def copy(self):
        s = Seg()
        s.w, s.rc, s.rd = self.w, dict(self.rc), list(self.rd)
        return s


class Tracker:
    def __init__(self):
        self.bounds = [0, 1 << 40]
        self.segs = [Seg()]

    def _split(self, x):
        i = bisect_right(self.bounds, x) - 1
        if self.bounds[i] == x:
            return i
        self.bounds.insert(i + 1, x)
        self.segs.insert(i + 1, self.segs[i].copy())
        return i + 1

    def access(self, lo, hi, op, write, deps):
        i = self._split(lo)
        j = self._split(hi)
        for s in self.segs[i:j]:
            if s.w is not None:
                deps.add(s.w)
            if write:
                deps.update(s.rc.values())
                deps.update(s.rd)
                s.rc, s.rd, s.w = {}, [], op
            else:
                if op.kind == "d":
                    s.rd.append(op)
                else:
                    s.rc[op.eng] = op


class Prog:
    def __init__(self):
        self.q = {e: [] for e in (PE, ACT, DVE, POOL, SP)}
        self.tr = Tracker()
        self.chan_count = {}
        self.last = {}

    def add(self, eng, kind, fn, reads, writes, chan=None, after=()):
        op = Op()
        op.eng, op.kind, op.fn, op.chan = eng, kind, fn, chan
        op.idx = len(self.q[eng])
        op.sig, op.val = False, 0
        deps = set()
        for v in reads:
            for lo, hi in v.ivs:
                self.tr.access(lo, hi, op, v.excl, deps)
        for v in writes:
            for lo, hi in v.ivs:
                self.tr.access(lo, hi, op, True, deps)
        deps.update(after)
        deps.discard(op)
        cw, dw = {}, {}
        for d in deps:
            if d.kind == "d":
                dw[d.chan] = self.chan_count[d.chan] * 16
            else:
                if d.eng == eng and kind == "c" and eng == PE:
                    continue
                if d.eng not in cw or cw[d.eng].idx < d.idx:
                    cw[d.eng] = d
        for d in cw.values():
            d.sig = True
        op.cwaits = list(cw.values())
        op.dwaits = list(dw.items())
        if kind == "d":
            self.chan_count[chan] = self.chan_count.get(chan, 0) + 1
        self.q[eng].append(op)
        self.last[eng] = op
        return op


def _prod(xs):
    r = 1
    for x in xs:
        r *= x
    return r


def build_program():
    nc = bass.Bass("TRN2", target_bir_lowering=False)
    P = Prog()

    def din(name, shape):
        return nc.dram_tensor(name, list(shape), F32, kind="ExternalInput").ap()

    def dout(name, shape):
        return nc.dram_tensor(name, list(shape), F32, kind="ExternalOutput").ap()

    x_d = din("x", [SEQ, D])
    xs_d = din("xs", [NS, D])
    spool_d = din("spool", [NS * 15, D])
    sconv_d = din("sconv", [NS * 2, D])
    ck_d = din("ck", [2, NS, NMEM, D])
    cv_d = din("cv", [2, NS, NMEM, D])
    mem_d = din("mem", [NMEM, D])
    gv_d = din("gv", [NGV, D])
    ident_d = din("ident", [128, 128])
    bands_d = din("bands", [128, NBC])
    wpool_d = din("w_pool", [4, 256, 256])
    wcin_d = din("w_conv_in", [D, 3 * D])
    wcout_d = din("w_conv_out", [D, D])
    wq_d = din("w_q", [2, D, D])
    wkv_d = din("w_kv", [2, D, 2 * D])
    wo_d = din("w_o", [2, D, D])
    wup_d = din("w_up", [2, D, 4 * D])
    wdn_d = din("w_down", [2, 4 * D, D])

    y_d = dout("y", [SEQ, D])
    ys_d = dout("ys", [NS, D])
    npp_d = dout("npp", [15, D])
    ncp_d = dout("ncp", [2, D])
    mk_d = dout("mk", [2, NMEM, D])
    mv_d = dout("mv", [2, NMEM, D])
    nps_d = dout("nps", [NS, 15, D])
    ncs_d = dout("ncs", [NS, 2, D])

    es = contextlib.ExitStack()
    MEM = es.enter_context(nc.sbuf_tensor("MEM", [128, MEM_BYTES // 2], BF16))
    PS = es.enter_context(nc.psum_tensor("PS", [128, 8, 512], F32))

    def DV(ap):
        return V(ap, [])

    class T:
        def __init__(self, off, dt, shape):
            assert off % 4 == 0
            self.off, self.dt, self.shape = off, dt, tuple(shape)
            self.esz = 4 if dt == F32 else 2
            n = _prod(shape)
            self.nbytes = n * self.esz
            assert off + self.nbytes <= MEM_BYTES, (off, self.nbytes)
            base = MEM[:, off // 2: (off + self.nbytes) // 2]
            if dt == F32:
                base = base.bitcast(F32)
            if len(shape) == 2:
                base = base.rearrange("p (a b) -> p a b", a=shape[0])
            elif len(shape) == 3:
                base = base.rearrange("p (a b c) -> p a b c", a=shape[0], b=shape[1])
            self.ap = base
            st, s = [], 1
            for d in reversed(shape):
                st.append(s)
                s *= d
            self.strides = tuple(reversed(st))

        def __call__(self, *idx, p=None):
            idx = list(idx) + [slice(None)] * (len(self.shape) - len(idx))
            rng = []
            for i, d in zip(idx, self.shape):
                if isinstance(i, int):
                    rng.append((i, i + 1))
                else:
                    a = 0 if i.start is None else i.start
                    b = d if i.stop is None else i.stop
                    assert 0 <= a < b <= d, (idx, self.shape)
                    rng.append((a, b))
            key = (slice(None) if p is None else slice(p[0], p[1]),) + tuple(idx)
            ap = self.ap[key]
            k = len(self.shape)
            while k > 1 and rng[k - 1] == (0, self.shape[k - 1]):
                k -= 1
            ivs = []

            def rec(dim, base):
                if dim == k - 1:
                    a, b = rng[dim]
                    lo = base + a * self.strides[dim]
                    hi = base + b * self.strides[dim]
                    ivs.append((self.off + lo * self.esz, self.off + hi * self.esz))
                    return
                for i in range(rng[dim][0], rng[dim][1]):
                    rec(dim + 1, base + i * self.strides[dim])
            rec(0, 0)
            return V(ap, ivs)

    def sl(a, b):
        return slice(a, b)

    def PSv(bank, a, b, m=128):
        return V(PS[0:m, bank, a:b], [(PSB + bank * 2048, PSB + (bank + 1) * 2048)], excl=True)

    def PSr(bank, pat, m=128, **kw):
        return V(PS[0:m, bank, :].rearrange(pat, **kw), [(PSB + bank * 2048, PSB + (bank + 1) * 2048)], excl=True)

    bank_ctr = [0]

    def nxt():
        b = bank_ctr[0] % 7
        bank_ctr[0] += 1
        return b

    top = [0]

    def palloc(dt, shape):
        t = T(top[0], dt, shape)
        top[0] += (t.nbytes + 63) // 64 * 64
        return t

    X = palloc(F32, [NCH, NTOK])
    U3 = palloc(BF16, [NCH, NTOK])
    WS = [palloc(BF16, [NCH, D]) for _ in range(3)]
    IDF = palloc(F32, [128])
    ONESB = palloc(BF16, [128])
    G = palloc(F32, [NCH, 16])
    INV = palloc(F32, [16])
    HALO = palloc(F32, [NCH, 2])
    PB = palloc(BF16, [NS, 2, 4])
    SS = palloc(F32, [NS, 2, 4])
    QS = palloc(BF16, [D])
    RDS = palloc(F32, [NS, 4])
    OTS = palloc(BF16, [NCH, NS])
    SCR0 = top[0]
    SCR_BYTES = MEM_BYTES - SCR0

    class Scratch:
        def __init__(self):
            self.top = SCR0

        def alloc(self, dt, shape):
            t = T(self.top, dt, shape)
            self.top += (t.nbytes + 63) // 64 * 64
            assert self.top <= MEM_BYTES, ("scratch overflow", self.top - SCR0, SCR_BYTES)
            return t

    def mm(out, lhsT, rhs, start, stop):
        P.add(PE, "c", lambda e, o=out.ap, l=lhsT.ap, r=rhs.ap: e.matmul(out=o, lhsT=l, rhs=r, start=start, stop=stop),
              [lhsT, rhs], [out])

    def mmx(out, lhsT, rhs, start, stop, is_transpose=False):
        P.add(PE, "c", lambda e, o=out.ap, l=lhsT.ap, r=rhs.ap: e.matmul(out=o, lhsT=l, rhs=r, start=start, stop=stop,
                                                                   is_transpose=is_transpose, skip_group_check=True),
              [lhsT, rhs], [out])

    def tp(out, in_, ident):
        P.add(PE, "c", lambda e, o=out.ap, i=in_.ap, d=ident.ap: e.transpose(out=o, in_=i, identity=d), [in_, ident], [out])

    def act(out, in_, func, scale=1.0, bias=0.0):
        rd = [in_]
        sc = scale
        if isinstance(scale, V):
            rd.append(scale)
            sc = scale.ap
        P.add(ACT, "c", lambda e, o=out.ap, i=in_.ap: e.activation(out=o, in_=i, func=func, bias=bias, scale=sc), rd, [out])

    def stt(out, in0, scalar, in1, op0, op1, eng=DVE):
        rd = [in0, in1]
        sc = scalar
        if isinstance(scalar, V):
            rd.append(scalar)
            sc = scalar.ap
        P.add(eng, "c", lambda e, o=out.ap, a=in0.ap, b=in1.ap: e.scalar_tensor_tensor(out=o, in0=a, scalar=sc, in1=b, op0=op0, op1=op1),
              rd, [out])

    def tt(out, in0, in1, op, eng=DVE):
        P.add(eng, "c", lambda e, o=out.ap, a=in0.ap, b=in1.ap: e.tensor_tensor(out=o, in0=a, in1=b, op=op), [in0, in1], [out])

    def tsc(out, in0, s1, op0, eng=DVE):
        rd = [in0]
        sc = s1
        if isinstance(s1, V):
            rd.append(s1)
            sc = s1.ap
        P.add(eng, "c", lambda e, o=out.ap, a=in0.ap: e.tensor_scalar(out=o, in0=a, scalar1=sc, scalar2=None, op0=op0), rd, [out])

    def cp(out, in_, eng=DVE):
        if eng == ACT:
            P.add(ACT, "c", lambda e, o=out.ap, i=in_.ap: e.copy(out=o, in_=i), [in_], [out])
        else:
            P.add(eng, "c", lambda e, o=out.ap, i=in_.ap: e.tensor_copy(out=o, in_=i), [in_], [out])

    def red(out, in_, eng=DVE):
        P.add(eng, "c", lambda e, o=out.ap, i=in_.ap: e.tensor_reduce(out=o, in_=i, axis=AX.X, op=ALU.add), [in_], [out])

    def memset(out, val, eng=DVE):
        P.add(eng, "c", lambda e, o=out.ap: e.memset(o, val), [], [out])

    def dma(q, out, in_, chan, after=()):
        P.add(q, "d", lambda e, o=out.ap, i=in_.ap: e.dma_start(out=o, in_=i), [in_], [out], chan=chan, after=after)

    evac_ctr = [0]

    def evac_eng():
        evac_ctr[0] += 1
        return ACT if evac_ctr[0] % 2 else DVE

    def wsrc_sq(ap2d):
        return ap2d.rearrange("(kc p) e -> p kc e", p=128)

    wlist = []
    for l in range(2):
        wlist.append((f"wk{l}", wsrc_sq(wkv_d[l][:, 0:D])))
        wlist.append((f"wv{l}", wsrc_sq(wkv_d[l][:, D:2 * D])))
        if l == 1:
            wlist.append(("cin_b", wsrc_sq(wcin_d[:, 0:D])))
            wlist.append(("cin_c", wsrc_sq(wcin_d[:, D:2 * D])))
            wlist.append(("cin_h", wsrc_sq(wcin_d[:, 2 * D:3 * D])))
            wlist.append(("cout", wsrc_sq(wcout_d)))
        wlist.append((f"wq{l}", wsrc_sq(wq_d[l])))
        wlist.append((f"wo{l}", wsrc_sq(wo_d[l])))
        for qq in range(4):
            wlist.append((f"up{l}_{qq}", wsrc_sq(wup_d[l][:, qq * D:(qq + 1) * D])))
            wlist.append((f"dn{l}_{qq}", wsrc_sq(wdn_d[l][qq * D:(qq + 1) * D, :])))
    widx = {n: i for i, (n, _) in enumerate(wlist)}
    wloaded = [0]

    def wload_next():
        i = wloaded[0]
        if i >= len(wlist):
            return
        wloaded[0] += 1
        dma(POOL, WS[i % 3](), DV(wlist[i][1]), f"w{i % 3}")

    def W(name):
        i = widx[name]
        assert i < wloaded[0], name
        return WS[i % 3]

    def wrelease(name):
        i = widx[name]
        while wloaded[0] <= i + 3 and wloaded[0] < len(wlist):
            wload_next()

    sc = Scratch()
    XS = [sc.alloc(F32, [D]) for _ in range(2)]
    GVS = sc.alloc(F32, [D])
    dma(SP, IDF(), DV(ident_d), "c")
    memset(ONESB(), 1.0)
    for j in range(15):
        memset(INV(sl(j, j + 1)), 1.0 / (j + 1))
    memset(HALO(), 0.0)

    def build_gains(GV2):
        bk = nxt()
        for c in range(NCH):
            tp(PSv(bk, c * 16, c * 16 + NGV), GV2(sl(c * 128, (c + 1) * 128), p=(0, NGV)), V(IDF.ap[0:NGV, 0:NGV], IDF().ivs))
        cp(G(sl(0, NCH), sl(0, NGV)), V(PS[:, bk, 0:128].rearrange("p (c r) -> p c r", c=NCH)[:, :, 0:NGV], PSv(bk, 0, 1).ivs, True))

    def mixer0_phase():
        s2 = Scratch()
        s2.top = sc.top
        XS3 = [XS[0], XS[1], GVS, s2.alloc(F32, [D])]
        NXS = 4
        SQJ = s2.alloc(F32, [D])
        UTM = [s2.alloc(BF16, [D]) for _ in range(3)]
        GBC = s2.alloc(F32, [D])
        PLB = [s2.alloc(BF16, [NCH, 128]) for _ in range(2)]
        BANDS = s2.alloc(BF16, [NBC])
        WP = s2.alloc(BF16, [2, 4, 256])
        UF = s2.alloc(F32, [D])
        UCF = s2.alloc(F32, [D])
        WPF = T(UF.off, F32, [2, 4, 256])
        assert UCF.off == UF.off + 4096
        STAT = s2.alloc(F32, [4, 4])
        STB = [s2.alloc(BF16, [D]) for _ in range(2)]
        UCB = s2.alloc(BF16, [D])
        PLS = s2.alloc(BF16, [NCH, NS])
        SCBv = T(SQJ.off, F32, [4, 256])

        def band(wi, kind):
            a = wi * 384 + kind * 128
            return BANDS(sl(a, a + 128))

        GV2 = T(STB[0].off, F32, [D])
        assert STB[1].off == STB[0].off + 2048
        dma(SP, GBC(), DV(gv_d[R_MIX:R_MIX + 1, :].broadcast_to([128, D])), "c2")
        for tb0 in range(2):
            dma(SP, XS3[tb0](), DV(x_d[tb0 * 128:(tb0 + 1) * 128, :]), f"xs{tb0}")
        dma(SP, SQJ(), DV(gv_d[R_PSCALE:R_PSCALE + 1, :].broadcast_to([128, D])), "c2")
        dma(POOL, BANDS(), DV(bands_d), "wp")
        dma(SP, GV2(p=(0, NGV)), DV(gv_d), "c")
        for cc in range(2):
            dma(SP, WPF(cc), DV(wpool_d[:, cc * 128:(cc + 1) * 128, :].rearrange("g p e -> p g e")), "c2")
        for cc in range(2):
            tt(WP(cc), WPF(cc), SCBv(), ALU.mult)

        def stat(i, col):
            return STAT(i % 4, sl(col, col + 1))

        def stA(tb):
            xs = XS3[tb % NXS]
            if tb >= 2:
                dma(SP, xs(), DV(x_d[tb * 128:(tb + 1) * 128, :]), f"xs{tb % NXS}")
            act(SQJ(), xs(), AF.Square)
            red(stat(tb, 0), SQJ())
            act(stat(tb, 1), stat(tb, 0), AF.Ln, scale=1.0 / D, bias=EPS)
            act(stat(tb, 2), stat(tb, 1), AF.Exp, scale=-0.5)
            stt(UTM[tb % 3](), xs(), stat(tb, 2), GBC(), ALU.mult, ALU.mult)
            if tb == 15:
                stt(UF(), xs(), stat(tb, 2), GBC(), ALU.mult, ALU.mult)
                dma(SP, DV(npp_d), UF(p=(113, 128)), "uf")

        def stB(tb):
            bks = [nxt(), nxt()]
            utm = UTM[tb % 3]
            for c in range(NCH):
                wi = c // 2
                out = PSv(bks[c // 4], (c % 4) * 128, (c % 4 + 1) * 128)
                if tb == 0:
                    mm(out, utm(sl(c * 128, (c + 1) * 128)), band(wi, 2), True, True)
                else:
                    mm(out, UTM[(tb - 1) % 3](sl(c * 128, (c + 1) * 128)), band(wi, 1), True, False)
                    mm(out, utm(sl(c * 128, (c + 1) * 128)), band(wi, 0), False, True)
            for half in range(2):
                cp(PLB[tb % 2](sl(half * 4, half * 4 + 4)), PSr(bks[half], "p (j t) -> p j t", j=4), eng=ACT)

        def stC(tb):
            bks = [nxt(), nxt()]
            pl = PLB[tb % 2]
            xs = XS3[tb % NXS]
            for c in range(NCH):
                mmx(PSv(bks[c // 4], (c % 4) * 128, (c % 4 + 1) * 128), xs(sl(c * 128, (c + 1) * 128)), IDF(),
                    c % 4 == 0, False, is_transpose=True)
            for g in range(4):
                for j in range(2):
                    ec = 2 * g + j
                    out = PSv(bks[ec // 4], (ec % 4) * 128, (ec % 4 + 1) * 128)
                    for cc in range(2):
                        mmx(out, WP(cc, g, sl(j * 128, (j + 1) * 128)), pl(2 * g + cc), False, cc == 1)
            for half in range(2):
                cp(X(sl(half * 4, half * 4 + 4), sl(tb * 128, (tb + 1) * 128)),
                   PSr(bks[half], "p (j t) -> p j t", j=4), eng=(ACT if half == 0 else DVE))

        pieces = [(wi_, kc) for wi_ in range(3) for kc in range(NCH)]

        def wpiece():
            if not pieces:
                return
            wi_, kc = pieces.pop(0)
            dma(POOL, WS[wi_ % 3](kc), DV(wlist[wi_][1][:, kc, :]), f"w{wi_ % 3}", after=((P.last[DVE],) if wi_ < 2 else ()))

        for step in range(16 + 2):
            if step < 16:
                stA(step)
                if step == 2:
                    build_gains(GV2)
                if step >= 1:
                    wpiece()
                    wpiece()
            if 0 <= step - 1 < 16:
                stB(step - 1)
            if 0 <= step - 2 < 16:
                stC(step - 2)

        while pieces:
            wpiece()
        wloaded[0] = 3
        for i in range(2):
            dma(SP, XS3[i](p=(0, 120)), DV(spool_d[i * 120:(i + 1) * 120, :]), f"xs{i}")
            cp(STB[i](p=(0, 120)), XS3[i](p=(0, 120)), eng=(ACT if i == 0 else DVE))
        dma(SP, DV(nps_d[:, 0:14, :]), DV(spool_d.rearrange("(b j) d -> b j d", j=15)[:, 1:15, :]), "d2d")
        xs = XS3[2]
        dma(SP, xs(p=(0, NS)), DV(xs_d), "xs2")
        bk = nxt()
        for c in range(NCH):
            tp(PSv(bk, c * 16, c * 16 + 16), xs(sl(c * 128, (c + 1) * 128), p=(0, NS)), V(IDF.ap[0:NS, 0:NS], IDF().ivs))
        cp(X(sl(0, NCH), sl(SEQ, NTOK)), V(PS[:, bk, 0:128].rearrange("p (c r) -> p c r", c=NCH), PSv(bk, 0, 1).ivs, True))
        P16 = (0, NS)
        act(SQJ(p=P16), xs(p=P16), AF.Square)
        red(STAT(0, sl(0, 1), p=P16), SQJ(p=P16))
        act(STAT(0, sl(1, 2), p=P16), STAT(0, sl(0, 1), p=P16), AF.Ln, scale=1.0 / D, bias=EPS)
        act(STAT(0, sl(2, 3), p=P16), STAT(0, sl(1, 2), p=P16), AF.Exp, scale=-0.5)
        stt(UCF(p=P16), xs(p=P16), STAT(0, sl(2, 3), p=P16), GBC(p=P16), ALU.mult, ALU.mult)
        cp(UCB(p=P16), UCF(p=P16), eng=ACT)
        dma(SP, DV(nps_d[:, 14, :]), UCF(p=P16), "ucf")
        bk = nxt()
        for c in range(NCH):
            wi = c // 2
            sb = 1536 + wi * 48
            out = PSv(bk, c * 16, c * 16 + 16)
            mm(out, STB[0](sl(c * 128, (c + 1) * 128), p=(0, 120)), BANDS(sl(sb, sb + 16), p=(0, 120)), True, False)
            mm(out, STB[1](sl(c * 128, (c + 1) * 128), p=(0, 120)), BANDS(sl(sb + 16, sb + 32), p=(0, 120)), False, False)
            mm(out, UCB(sl(c * 128, (c + 1) * 128), p=P16), BANDS(sl(sb + 32, sb + 48), p=P16), False, True)
        cp(PLS(), V(PS[:, bk, 0:128].rearrange("p (c b) -> p c b", c=NCH), PSv(bk, 0, 1).ivs, True), eng=ACT)
        bk = nxt()
        for g in range(4):
            for j in range(2):
                ec = 2 * g + j
                for cc in range(2):
                    mm(PSv(bk, ec * 16, ec * 16 + 16), WP(cc, g, sl(j * 128, (j + 1) * 128)), PLS(2 * g + cc), cc == 0, cc == 1)
        xv = X(sl(0, NCH), sl(SEQ, NTOK))
        tt(xv, V(PS[:, bk, 0:128].rearrange("p (c b) -> p c b", c=NCH), PSv(bk, 0, 1).ivs, True), xv, ALU.add)

    def norm_a(c0, n, SQ):
        act(SQ(sl(0, NCH), sl(0, n)), X(sl(0, NCH), sl(c0, c0 + n)), AF.Square)

    def norm_b(c0, n, grow, outv, SQ, RSTD, extra=None):
        bk = nxt()
        for c in range(NCH):
            mm(PSv(bk, 0, n), ONESB(), SQ(c, sl(0, n)), c == 0, c == NCH - 1)
        act(PSv(bk, 0, n), PSv(bk, 0, n), AF.Ln, scale=1.0 / D, bias=EPS)
        act(RSTD(sl(0, n)), PSv(bk, 0, n), AF.Exp, scale=-0.5)
        for c in range(NCH):
            stt(outv(c), X(c, sl(c0, c0 + n)), G(c, sl(grow, grow + 1)), RSTD(sl(0, n)), ALU.mult, ALU.mult)
        if extra is not None:
            extra(RSTD)

    def norm(c0, n, grow, outv, SQ, RSTD, extra=None):
        norm_a(c0, n, SQ)
        norm_b(c0, n, grow, outv, SQ, RSTD, extra)

    def kv_phase(l, KT, VV):
        o = U3.off
        ST = [T(o + i * 4096, F32, [D]) for i in range(3)]
        MEMT = T(o + 12288, F32, [NCH, NMEM])
        SQM = T(o + 20480, BF16, [NCH, NMEM])
        MEMN = T(o + 24576, BF16, [NCH, NMEM])
        RSM = T(o + 28672, F32, [NMEM])
        assert 28672 + 1024 <= U3.nbytes
        for nb in range(2):
            dma(SP, ST[nb](), DV(mem_d[nb * 128:(nb + 1) * 128, :]), f"st{nb}")
            for half in range(2):
                bk = nxt()
                for j in range(4):
                    c = half * 4 + j
                    tp(PSv(bk, j * 128, (j + 1) * 128), ST[nb](sl(c * 128, (c + 1) * 128)), IDF())
                cp(MEMT(sl(half * 4, half * 4 + 4), sl(nb * 128, (nb + 1) * 128)),
                   PSr(bk, "p (j t) -> p j t", j=4), eng=evac_eng())
        act(SQM(), MEMT(), AF.Square)
        bk = nxt()
        for c in range(NCH):
            mm(PSv(bk, 0, NMEM), ONESB(), SQM(c), c == 0, c == NCH - 1)
        act(PSv(bk, 0, NMEM), PSv(bk, 0, NMEM), AF.Ln, scale=1.0 / D, bias=EPS)
        act(RSM(), PSv(bk, 0, NMEM), AF.Exp, scale=-0.5)
        for c in range(NCH):
            stt(MEMN(c), MEMT(c), G(c, sl(R_MEM + l, R_MEM + l + 1)), RSM(), ALU.mult, ALU.mult)
        yield
        Wk, Wv = W(f"wk{l}"), W(f"wv{l}")
        for ep in range(4):
            bk = nxt()
            for j in range(2):
                ec = ep * 2 + j
                for kc in range(NCH):
                    mm(PSv(bk, j * 256, (j + 1) * 256), Wk(kc, sl(ec * 128, (ec + 1) * 128)), MEMN(kc), kc == 0, kc == NCH - 1)
            cp(KT(sl(ep * 2, ep * 2 + 2)), PSr(bk, "p (j n) -> p j n", j=2), eng=evac_eng())
            yield
        si = 0
        for which, Wt, od in ((0, Wk, mk_d), (1, Wv, mv_d)):
            for nb in range(2):
                st = ST[si % 3]
                for half in range(2):
                    bk = nxt()
                    for kc in range(NCH):
                        mm(PSv(bk, 0, 512), MEMN(kc, sl(nb * 128, (nb + 1) * 128)), Wt(kc, sl(half * 512, (half + 1) * 512)),
                           kc == 0, kc == NCH - 1)
                    cp(st(sl(half * 512, (half + 1) * 512)), PSv(bk, 0, 512), eng=ACT)
                    if which == 1:
                        cp(VV(nb, sl(half * 512, (half + 1) * 512)), st(sl(half * 512, (half + 1) * 512)), eng=DVE)
                    yield
                dma(SP, DV(od[l, nb * 128:(nb + 1) * 128, :]), st(), f"st{si % 3}")
                si += 1
        wrelease(f"wk{l}")
        wrelease(f"wv{l}")

    kv_scr_top = [SCR0]

    KT = palloc(BF16, [NCH, NMEM])
    VV = palloc(BF16, [2, D])
    SCRB = top[0]

    class Scratch2(Scratch):
        def __init__(self):
            self.top = SCRB
    kv_scr_top[0] = SCRB

    def pool_phase(kvgen):
        s2 = Scratch2()
        WP = s2.alloc(BF16, [2, 4, 256])
        SQ = s2.alloc(BF16, [NCH, TW])
        PL = s2.alloc(BF16, [NCH, TW])
        UBS = [s2.alloc(BF16, [NCH, 16 + TW]) for _ in range(2)]
        RSTDS = [s2.alloc(F32, [TW]) for _ in range(2)]
        U15 = s2.alloc(F32, [NCH, 16])
        FX = s2.alloc(F32, [16])
        SW = s2.alloc(F32, [NCH, NS])
        TA = [s2.alloc(F32, [16 + TW]) for _ in range(4)]
        TB = [s2.alloc(F32, [16 + TW]) for _ in range(4)]
        STG = [T(U3.off + i * 4096, F32, [D]) for i in range(2)]
        SPT = T(U3.off + 8192, F32, [NCH, NS, 16])
        for cc in range(2):
            dma(POOL, WP(cc), DV(wpool_d[:, cc * 128:(cc + 1) * 128, :].rearrange("g p e -> p g e")), "wp")
        memset(UBS[0](sl(0, NCH), sl(0, 16)), 0.0)
        dma(SP, STG[0](), DV(spool_d[0:128, :]), "stg0")
        dma(SP, STG[1](p=(0, 112)), DV(spool_d[128:240, :]), "stg1")
        for cpair in range(4):
            bk = nxt()
            for j in range(2):
                c = cpair * 2 + j
                tp(PSv(bk, j * 256, j * 256 + 128), STG[0](sl(c * 128, (c + 1) * 128)), IDF())
                tp(PSv(bk, j * 256 + 128, j * 256 + 240), STG[1](sl(c * 128, (c + 1) * 128), p=(0, 112)),
                   V(IDF.ap[0:112, 0:112], IDF().ivs))
            for j in range(2):
                c = cpair * 2 + j
                cp(SPT(c, sl(0, NS), sl(0, 15)),
                   V(PS[:, bk, j * 256:j * 256 + 240].rearrange("p (b j) -> p b j", b=NS), PSv(bk, 0, 1).ivs, True),
                   eng=evac_eng())
        dma(SP, DV(nps_d[:, 0:14, :]), DV(spool_d.rearrange("(b j) d -> b j d", j=15)[:, 1:15, :]), "d2d")

        def do_norm(ti):
            c0, n, npmt, ns = TILES[ti]
            last = ti == NT - 1
            UBt = UBS[ti % 2]

            def extra(RS, c0=c0, n=n, npmt=npmt):
                for c in range(NCH):
                    stt(SPT(c, sl(0, NS), 15), X(c, sl(c0 + npmt, c0 + n)), G(c, sl(R_MIX, R_MIX + 1)),
                        RS(sl(npmt, n)), ALU.mult, ALU.mult)
                    stt(U15(c, sl(0, 15)), X(c, sl(c0 + npmt - 15, c0 + npmt)), G(c, sl(R_MIX, R_MIX + 1)),
                        RS(sl(npmt - 15, npmt)), ALU.mult, ALU.mult)
            if ti > 0:
                pn = TILES[ti - 1][2]
                cp(UBt(sl(0, NCH), sl(0, 16)), UBS[(ti - 1) % 2](sl(0, NCH), sl(pn, pn + 16)))
            norm(c0, n, R_MIX + 0, lambda c, n=n: UBt(c, sl(16, 16 + n)), SQ, RSTDS[ti % 2], extra if last else None)

        do_norm(0)
        for ti in range(NT):
            c0, n, npmt, ns = TILES[ti]
            last = ti == NT - 1
            E = 16 + npmt
            UB = UBS[ti % 2]
            if not last:
                do_norm(ti + 1)
            for c in range(NCH):
                g = c // 2
                L = g + 1
                w = 1 << L
                we = POOL if c >= 4 else DVE
                pi = 0 if c < 4 else 1 + (c - 4) % 3
                ta, tb = TA[pi], TB[pi]
                tt(ta(sl(2, E)), UB(c, sl(2, E)), UB(c, sl(1, E - 1)), ALU.add, eng=we)
                if L >= 2:
                    tt(tb(sl(4, E)), ta(sl(4, E)), ta(sl(2, E - 2)), ALU.add, eng=we)
                if L >= 3:
                    tt(ta(sl(8, E)), tb(sl(8, E)), tb(sl(4, E - 4)), ALU.add, eng=we)
                if L >= 4:
                    tt(tb(sl(16, E)), ta(sl(16, E)), ta(sl(8, E - 8)), ALU.add, eng=we)
                al = ta if L in (1, 3) else tb
                stt(PL(c, sl(0, npmt)), al(sl(16, E)), 1.0 / w, UB(c, sl(16, E)), ALU.mult, ALU.subtract)
                if ti == 0:
                    tt(FX(sl(0, w - 1)), al(sl(16, 16 + w - 1)), INV(sl(0, w - 1)), ALU.mult)
                    tt(PL(c, sl(0, w - 1)), FX(sl(0, w - 1)), UB(c, sl(16, 16 + w - 1)), ALU.subtract)
            if last:
                for g in range(4):
                    w = 1 << (g + 1)
                    red(SW(sl(2 * g, 2 * g + 2), sl(0, NS)), SPT(sl(2 * g, 2 * g + 2), sl(0, NS), sl(16 - w, 16)))
                    stt(PL(sl(2 * g, 2 * g + 2), sl(npmt, n)), SW(sl(2 * g, 2 * g + 2), sl(0, NS)), 1.0 / w,
                        SPT(sl(2 * g, 2 * g + 2), sl(0, NS), 15), ALU.mult, ALU.subtract)
            for g in range(4):
                for j in range(2):
                    ec = 2 * g + j
                    bk = nxt()
                    for cc in range(2):
                        mm(PSv(bk, 0, n), WP(cc, g, sl(j * 128, (j + 1) * 128)), PL(2 * g + cc, sl(0, n)), cc == 0, cc == 1)
                    stt(X(ec, sl(c0, c0 + n)), PSv(bk, 0, n), G(ec, sl(R_PSCALE, R_PSCALE + 1)), X(ec, sl(c0, c0 + n)),
                        ALU.mult, ALU.add)
            for _ in range(3):
                next(kvgen, None)
        for _ in kvgen:
            pass
        bks = [nxt(), nxt()]
        for c in range(NCH):
            tp(PSv(bks[c // 4], (c % 4) * 128, (c % 4 + 1) * 128, m=15), U15(c, sl(0, 15)), IDF())
        for hf in range(2):
            cp(STG[0](sl(hf * 512, (hf + 1) * 512), p=(0, 15)), PSv(bks[hf], 0, 512, m=15), eng=evac_eng())
        dma(SP, DV(npp_d), STG[0](p=(0, 15)), "stg0")
        bks = [nxt(), nxt()]
        for c in range(NCH):
            tp(PSv(bks[c // 4], (c % 4) * 128, (c % 4 + 1) * 128, m=NS), SPT(c, sl(0, NS), 15), IDF())
        for hf in range(2):
            cp(STG[1](sl(hf * 512, (hf + 1) * 512), p=(0, NS)), PSv(bks[hf], 0, 512, m=NS), eng=evac_eng())
        dma(SP, DV(nps_d[:, 14, :]), STG[1](p=(0, NS)), "stg1")

    def conv_phase():
        s2 = Scratch2()
        SQ = s2.alloc(BF16, [NCH, TW])
        RSTD = [s2.alloc(F32, [TW]) for _ in range(2)]
        UT = [s2.alloc(BF16, [NCH, TW]) for _ in range(2)]
        GC = [s2.alloc(F32, [TW]) for _ in range(2)]
        ACC = [s2.alloc(F32, [TW]) for _ in range(2)]
        CH = [s2.alloc(F32, [2 + TW]) for _ in range(2)]
        SCT = s2.alloc(F32, [NCH, NS, 2])
        NCt = s2.alloc(F32, [NCH, 2])
        CHS = s2.alloc(F32, [NCH, NS])
        STG = [s2.alloc(F32, [D]) for _ in range(2)]
        Wb, Wc, Wh = W("cin_b"), W("cin_c"), W("cin_h")
        dma(SP, STG[0](p=(0, 32)), DV(sconv_d), "stg0")
        bk = nxt()
        for c in range(NCH):
            tp(PSv(bk, c * 32, (c + 1) * 32), STG[0](sl(c * 128, (c + 1) * 128), p=(0, 32)), V(IDF.ap[0:32, 0:32], IDF().ivs))
        cp(SCT(), V(PS[:, bk, 0:256].rearrange("p (c b j) -> p c b j", c=NCH, b=NS), PSv(bk, 0, 1).ivs, True))
        dma(SP, DV(ncs_d[:, 0, :]), DV(sconv_d.rearrange("(b j) d -> b j d", j=2)[:, 1, :]), "d2d")

        def nA(ti):
            c0, n, npmt, ns = TILES[ti]
            norm_a(c0, n, SQ)

        def nB(ti):
            c0, n, npmt, ns = TILES[ti]
            norm_b(c0, n, R_MIX + 1, lambda c, n=n, ti=ti: UT[ti % 2](c, sl(0, n)), SQ, RSTD[ti % 2])

        def stB(ti, mid=None):
            c0, n, npmt, ns = TILES[ti]
            last = ti == NT - 1
            ut = UT[ti % 2]
            for c in range(NCH):
                if c == 4 and mid is not None:
                    mid()
                k = c % 2
                bb, bc_, bh = nxt(), nxt(), nxt()
                for bkx, Wt in ((bc_, Wc), (bh, Wh), (bb, Wb)):
                    for kc in range(NCH):
                        mm(PSv(bkx, 0, n), Wt(kc, sl(c * 128, (c + 1) * 128)), ut(kc, sl(0, n)), kc == 0, kc == NCH - 1)
                cp(GC[k](sl(0, n)), PSv(bc_, 0, n), eng=ACT)
                cp(CH[k](sl(0, 2)), HALO(c))
                tt(CH[k](sl(2, 2 + n)), PSv(bh, 0, n), GC[k](sl(0, n)), ALU.mult)
                act(ACC[k](sl(0, n)), CH[k](sl(2, 2 + n)), AF.Copy, scale=G(c, sl(R_WCONV + 2, R_WCONV + 3)))
                stt(ACC[k](sl(0, npmt)), CH[k](sl(1, 1 + npmt)), G(c, sl(R_WCONV + 1, R_WCONV + 2)), ACC[k](sl(0, npmt)),
                    ALU.mult, ALU.add)
                stt(ACC[k](sl(0, npmt)), CH[k](sl(0, npmt)), G(c, sl(R_WCONV + 0, R_WCONV + 1)), ACC[k](sl(0, npmt)),
                    ALU.mult, ALU.add)
                if last:
                    stt(ACC[k](sl(npmt, n)), SCT(c, sl(0, NS), 1), G(c, sl(R_WCONV + 1, R_WCONV + 2)), ACC[k](sl(npmt, n)),
                        ALU.mult, ALU.add)
                    stt(ACC[k](sl(npmt, n)), SCT(c, sl(0, NS), 0), G(c, sl(R_WCONV + 0, R_WCONV + 1)), ACC[k](sl(npmt, n)),
                        ALU.mult, ALU.add)
                    cp(NCt(c), CH[k](sl(npmt, npmt + 2)))
                    cp(CHS(c), CH[k](sl(2 + npmt, 2 + n)))
                else:
                    cp(HALO(c), CH[k](sl(npmt, npmt + 2)))
                tt(U3(c, sl(c0, c0 + n)), PSv(bb, 0, n), ACC[k](sl(0, n)), ALU.mult)

        nA(0)
        nB(0)
        nA(1)
        for ti in range(NT):
            if ti + 1 < NT:
                nB(ti + 1)
            stB(ti, (lambda ti=ti: nA(ti + 2)) if ti + 2 < NT else None)
        wrelease("cin_b")
        wrelease("cin_c")
        wrelease("cin_h")
        bks = [nxt(), nxt()]
        for c in range(NCH):
            tp(PSv(bks[c // 4], (c % 4) * 128, (c % 4 + 1) * 128, m=2), NCt(c), IDF())
        for hf in range(2):
            cp(STG[0](sl(hf * 512, (hf + 1) * 512), p=(0, 2)), PSv(bks[hf], 0, 512, m=2), eng=evac_eng())
        dma(SP, DV(ncp_d), STG[0](p=(0, 2)), "stg0")
        bks = [nxt(), nxt()]
        for c in range(NCH):
            tp(PSv(bks[c // 4], (c % 4) * 128, (c % 4 + 1) * 128, m=NS), CHS(c), IDF())
        for hf in range(2):
            cp(STG[1](sl(hf * 512, (hf + 1) * 512), p=(0, NS)), PSv(bks[hf], 0, 512, m=NS), eng=evac_eng())
        dma(SP, DV(ncs_d[:, 1, :]), STG[1](p=(0, NS)), "stg1")
        Wco = W("cout")
        for ti in range(NT):
            c0, n, npmt, ns = TILES[ti]
            for ec in range(NCH):
                bk = nxt()
                for kc in range(NCH):
                    mm(PSv(bk, 0, n), Wco(kc, sl(ec * 128, (ec + 1) * 128)), U3(kc, sl(c0, c0 + n)), kc == 0, kc == NCH - 1)
                tt(X(ec, sl(c0, c0 + n)), PSv(bk, 0, n), X(ec, sl(c0, c0 + n)), ALU.add)
        wrelease("cout")

    def attn_phase(l):
        s2 = Scratch2()
        SQ = s2.alloc(BF16, [NCH, TW])
        QT = s2.alloc(BF16, [NCH, TW])
        BT = [s2.alloc(BF16, [NCH, TW]) for _ in range(2)]
        PT = [s2.alloc(BF16, [2, TW]) for _ in range(2)]
        RSTD = [s2.alloc(F32, [TW]) for _ in range(2)]
        RD = [s2.alloc(F32, [TW]) for _ in range(2)]
        KS = [T(U3.off + i * 4096, F32, [D]) for i in range(4)]
        VS = [T(U3.off + 16384 + i * 2048, BF16, [2, 512]) for i in range(4)]
        PRf = [T(U3.off + 24576 + i * 4096, F32, [D]) for i in range(2)]
        PR = [T(U3.off + 24576 + i * 4096, F32, [4, 256]) for i in range(2)]
        assert 24576 + 8192 <= U3.nbytes
        QBC = [s2.alloc(BF16, [D]) for _ in range(2)]
        SEL = s2.alloc(BF16, [NS, 128])
        RDSf = T(RDS.off, F32, [NS * 4])
        Wq, Wo = W(f"wq{l}"), W(f"wo{l}")
        cp(SEL(p=(0, NS)), V(IDF.ap[0:NS, 0:NS].unsqueeze(2).broadcast_to([NS, NS, 128]), IDF().ivs))
        seq = [4, 0, 1, 2, 3]

        def N2(k):
            N2a(k)
            N2b(k)

        def N2a(k):
            c0, n, npmt, ns = TILES[seq[k]]
            norm_a(c0, n, SQ)

        def N2b(k):
            c0, n, npmt, ns = TILES[seq[k]]
            norm_b(c0, n, R_ATTN + l, lambda c, n=n, k=k: BT[k % 2](c, sl(0, n)), SQ, RSTD[k % 2])

        def Q(k, mid=None):
            ti = seq[k]
            c0, n, npmt, ns = TILES[ti]
            ut = BT[k % 2]
            for ec in range(NCH):
                bk = nxt()
                for kc in range(NCH):
                    mm(PSv(bk, 0, n), Wq(kc, sl(ec * 128, (ec + 1) * 128)), ut(kc, sl(0, n)), kc == 0, kc == NCH - 1)
                cp(QT(ec, sl(0, n)), PSv(bk, 0, n), eng=ACT)
                if ec == 3 and mid is not None:
                    mid()
            if ns:
                for hf in range(2):
                    bk = nxt()
                    for kc in range(NCH):
                        mm(PSv(bk, 0, 512, m=NS), ut(kc, sl(npmt, n)), Wq(kc, sl(hf * 512, (hf + 1) * 512)), kc == 0, kc == NCH - 1)
                    cp(QS(sl(hf * 512, (hf + 1) * 512), p=(0, NS)), PSv(bk, 0, 512, m=NS), eng=ACT)

        def ATT(k, slot=None):
            ti = seq[k]
            c0, n, npmt, ns = TILES[ti]
            ot = BT[k % 2]

            def S(h):
                pt = PT[h % 2]
                for ncx in range(2):
                    bk = nxt()
                    for j in range(2):
                        mm(PSv(bk, 0, npmt), KT(2 * h + j, sl(ncx * 128, (ncx + 1) * 128)), QT(2 * h + j, sl(0, npmt)), j == 0, j == 1)
                    act(pt(ncx, sl(0, npmt)), PSv(bk, 0, npmt), AF.Exp, scale=1.0 / 16.0)

            def PVh(h):
                pt = PT[h % 2]
                bo = [nxt(), nxt()]
                bd = nxt()
                for ncx in range(2):
                    mm(PSv(bd, 0, npmt), ONESB(), pt(ncx, sl(0, npmt)), ncx == 0, ncx == 1)
                for j in range(2):
                    for ncx in range(2):
                        mm(PSv(bo[j], 0, npmt), VV(ncx, sl((2 * h + j) * 128, (2 * h + j + 1) * 128)), pt(ncx, sl(0, npmt)),
                           ncx == 0, ncx == 1)
                act(PSv(bd, 0, npmt), PSv(bd, 0, npmt), AF.Ln)
                act(RD[h % 2](sl(0, npmt)), PSv(bd, 0, npmt), AF.Exp, scale=-1.0)
                for j in range(2):
                    tt(ot(2 * h + j, sl(0, npmt)), PSv(bo[j], 0, npmt), RD[h % 2](sl(0, npmt)), ALU.mult)

            S(0)
            for h in range(4):
                if h + 1 < 4:
                    S(h + 1)
                PVh(h)
                if slot is not None:
                    slot(h)

        def O(k, mid=None):
            ti = seq[k]
            c0, n, npmt, ns = TILES[ti]
            ot = BT[k % 2]
            for ec in range(NCH):
                bk = nxt()
                for kc in range(NCH):
                    mm(PSv(bk, 0, npmt), Wo(kc, sl(ec * 128, (ec + 1) * 128)), ot(kc, sl(0, npmt)), kc == 0, kc == NCH - 1)
                tt(X(ec, sl(c0, c0 + npmt)), PSv(bk, 0, npmt), X(ec, sl(c0, c0 + npmt)), ALU.add)
                if ec == 3 and mid is not None:
                    mid()

        def SD(b):
            for ncx in range(2):
                i = (2 * b + ncx) % 4
                dma(SP, KS[i](), DV(ck_d[l, b, ncx * 128:(ncx + 1) * 128, :]), f"ks{i}")
            for hf in range(2):
                i = (2 * b + hf) % 4
                dma(POOL, VS[i](), DV(cv_d[l, b].rearrange("(nc p) e -> p nc e", p=128)[:, :, hf * 512:(hf + 1) * 512]), f"vs{i}")

        def SA(b):
            qb = QBC[b % 2]
            for h2 in range(2):
                bk = nxt()
                mm(PSv(bk, 0, 512), SEL(b, p=(0, NS)), QS(sl(h2 * 512, (h2 + 1) * 512), p=(0, NS)), True, True)
                cp(qb(sl(h2 * 512, (h2 + 1) * 512)), PSv(bk, 0, 512), eng=ACT)
            for ncx in range(2):
                i = (2 * b + ncx) % 4
                tt(PRf[ncx](), KS[i](), qb(), ALU.mult)
                red(SS(b, ncx), PR[ncx]())

        def SE(b):
            act(PB(b), SS(b), AF.Exp, scale=1.0 / 16.0)

        def SB(b):
            for ncx in range(2):
                mm(PSv(7, 128 + b * 4, 128 + b * 4 + 4), ONESB(), PB(b, ncx), ncx == 0, ncx == 1)
            for hf in range(2):
                i = (2 * b + hf) % 4
                for cc in range(4):
                    c = hf * 4 + cc
                    h = c // 2
                    for ncx in range(2):
                        mm(PSv(7, c * 16 + b, c * 16 + b + 1), VS[i](ncx, sl(cc * 128, (cc + 1) * 128)), PB(b, ncx, sl(h, h + 1)),
                           ncx == 0, ncx == 1)

        N2(0)
        Q(0)
        SD(0)
        sctr = [0]

        def sample_slot():
            b = sctr[0]
            if 0 < b <= NS:
                SE(b - 1)
                SB(b - 1)
            if b + 1 < NS:
                SD(b + 1)
            if b < NS:
                SA(b)
            sctr[0] += 1

        def make_slot(k):
            def slot(h):
                if h == 0 and k + 1 < 5:
                    N2a(k + 1)
                if h == 1 and k + 1 < 5:
                    N2b(k + 1)
                if h == 1 or h == 3:
                    sample_slot()
            return slot

        for k in range(5):
            ATT(k, make_slot(k))
            O(k, sample_slot)
            if k + 1 < 5:
                Q(k + 1, sample_slot)
        while sctr[0] <= NS:
            sample_slot()
        assert sctr[0] > NS
        wrelease(f"wq{l}")

        def finish():
            act(PSv(7, 128, 192), PSv(7, 128, 192), AF.Ln)
            act(RDSf(), PSv(7, 128, 192), AF.Exp, scale=-1.0)
            for c in range(NCH):
                tt(OTS(c), PSv(7, c * 16, c * 16 + 16), RDS(sl(0, NS), c // 2), ALU.mult)
            for ec in range(NCH):
                bk = nxt()
                for kc in range(NCH):
                    mm(PSv(bk, 0, NS), Wo(kc, sl(ec * 128, (ec + 1) * 128)), OTS(kc), kc == 0, kc == NCH - 1)
                tt(X(ec, sl(SEQ, NTOK)), PSv(bk, 0, NS), X(ec, sl(SEQ, NTOK)), ALU.add)
            wrelease(f"wo{l}")
        return finish

    def mlp_phase(l, finish_attn):
        fin = l == 1
        s2 = Scratch() if fin else Scratch2()
        SQ = s2.alloc(BF16, [NCH, TW])
        RSTD = [s2.alloc(F32, [TW]) for _ in range(2)]
        H = [s2.alloc(BF16, [NCH, TW]) for _ in range(2)]
        R = [s2.alloc(F32, [TW]) for _ in range(4)]
        if fin:
            STG = [s2.alloc(F32, [D]) for _ in range(2)]
            GB = s2.alloc(F32, [D])
            SQT = s2.alloc(F32, [D])
            SSs = s2.alloc(F32, [4])
            dma(SP, GB(), DV(gv_d[R_FINAL:R_FINAL + 1, :].broadcast_to([128, D])), "gb")
        rc = [0]
        stg_ctr = [0]

        def N3(ti):
            c0, n, npmt, ns = TILES[ti]
            if ti == NT - 1:
                finish_attn()
            norm(c0, n, R_FFN + l, lambda c, c0=c0, n=n: U3(c, sl(c0, c0 + n)), SQ, RSTD[ti % 2])

        def stA(q, ti, j):
            c0, n, npmt, ns = TILES[ti]
            Wu = W(f"up{l}_{q}")
            h = H[j % 2]
            for fc in range(NCH):
                bk = nxt()
                for kc in range(NCH):
                    mm(PSv(bk, 0, n), Wu(kc, sl(fc * 128, (fc + 1) * 128)), U3(kc, sl(c0, c0 + n)), kc == 0, kc == NCH - 1)
                r = R[rc[0] % 4]
                rc[0] += 1
                act(r(sl(0, n)), PSv(bk, 0, n), AF.Relu)
                if fin and q == 3:
                    act(h(fc, sl(0, n)), r(sl(0, n)), AF.Square)
                else:
                    tt(h(fc, sl(0, n)), r(sl(0, n)), r(sl(0, n)), ALU.mult)
            if ti == NT - 1:
                wrelease(f"up{l}_{q}")

        def stB(q, ti, j):
            c0, n, npmt, ns = TILES[ti]
            Wd = W(f"dn{l}_{q}")
            h = H[j % 2]
            for ec in range(NCH):
                bk = nxt()
                for fc in range(NCH):
                    mm(PSv(bk, 0, n), Wd(fc, sl(ec * 128, (ec + 1) * 128)), h(fc, sl(0, n)), fc == 0, fc == NCH - 1)
                tt(X(ec, sl(c0, c0 + n)), PSv(bk, 0, n), X(ec, sl(c0, c0 + n)), ALU.add)
            if ti == NT - 1:
                wrelease(f"dn{l}_{q}")

        def final_tile(ti):
            c0, n, npmt, ns = TILES[ti]
            blocks = [(a, min(npmt, a + 128), False) for a in range(0, npmt, 128)]
            if ns:
                blocks.append((npmt, n, True))
            for a, b, is_s in blocks:
                m = b - a
                bks = [nxt(), nxt()]
                for c in range(NCH):
                    tp(PSv(bks[c // 4], (c % 4) * 128, (c % 4 + 1) * 128, m=m), X(c, sl(c0 + a, c0 + b)), IDF())
                k = stg_ctr[0] % 2
                stg_ctr[0] += 1
                st = STG[k]
                for hf in range(2):
                    act(SQT(sl(hf * 512, (hf + 1) * 512), p=(0, m)), PSv(bks[hf], 0, 512, m=m), AF.Square)
                red(SSs(sl(0, 1), p=(0, m)), SQT(p=(0, m)))
                act(SSs(sl(1, 2), p=(0, m)), SSs(sl(0, 1), p=(0, m)), AF.Ln, scale=1.0 / D, bias=EPS)
                act(SSs(sl(2, 3), p=(0, m)), SSs(sl(1, 2), p=(0, m)), AF.Exp, scale=-0.5)
                for hf in range(2):
                    stt(st(sl(hf * 512, (hf + 1) * 512), p=(0, m)), PSv(bks[hf], 0, 512, m=m), SSs(sl(2, 3), p=(0, m)),
                        GB(sl(hf * 512, (hf + 1) * 512), p=(0, m)), ALU.mult, ALU.mult)
                if is_s:
                    dma(SP, DV(ys_d), st(p=(0, m)), f"stg{k}")
                else:
                    dma(SP, DV(y_d[c0 + a:c0 + b, :]), st(p=(0, m)), f"stg{k}")

        items = [(q, ti) for q in range(4) for ti in range(NT)]
        N3(0)
        N3(1)
        stA(0, 0, 0)
        pend = []
        for j, (q, ti) in enumerate(items):
            if q == 0 and ti + 2 < NT:
                N3(ti + 2)
            if j + 1 < len(items):
                stA(items[j + 1][0], items[j + 1][1], j + 1)
            if fin and pend:
                final_tile(pend.pop(0))
            stB(q, ti, j)
            if fin and q == 3:
                pend.append(ti)
                if ti == NT - 1:
                    while pend:
                        final_tile(pend.pop(0))
        while fin and pend:
            final_tile(pend.pop(0))

    mixer0_phase()
    for _ in kv_phase(0, KT, VV):
        pass
    fin = attn_phase(0)
    mlp_phase(0, fin)
    for _ in kv_phase(1, KT, VV):
        pass
    conv_phase()
    fin = attn_phase(1)
    mlp_phase(1, fin)

    sems = {e: es.enter_context(nc.semaphore(f"s_{e}")) for e in (PE, ACT, DVE, POOL)}
    chans = {c: es.enter_context(nc.semaphore(f"c_{c}")) for c in P.chan_count}
    for e in (PE, ACT, DVE, POOL, SP):
        cnt = 0
        for op in P.q[e]:
            if op.kind == "c" and op.sig:
                cnt += 1
            op.val = cnt

    def emit(name, e):
        known = {}
        for op in P.q[name]:
            for d in op.cwaits:
                key = ("e", d.eng)
                if known.get(key, 0) < d.val:
                    e.wait_ge(sems[d.eng], d.val)
                    known[key] = d.val
            for ch, val in op.dwaits:
                key = ("c", ch)
                if known.get(key, 0) < val:
                    e.wait_ge(chans[ch], val)
                    known[key] = val
            ins = op.fn(e)
            if op.kind == "d":
                ins.then_inc(chans[op.chan], 16)
            elif op.sig:
                ins.then_inc(sems[name], 1)
        if name == SP:
            for ch, cnt in P.chan_count.items():
                if known.get(("c", ch), 0) < cnt * 16:
                    e.wait_ge(chans[ch], cnt * 16)

    with nc.Block() as block:
        @block.tensor
        def _(e):
            emit(PE, e)

        @block.scalar
        def _(e):
            emit(ACT, e)

        @block.vector
        def _(e):
            emit(DVE, e)

        @block.gpsimd
        def _(e):
            emit(POOL, e)

        @block.sync
        def _(e):
            emit(SP, e)
    es.close()
    stats = {k: len(v) for k, v in P.q.items()}
    return nc, stats


_CACHE = {}


def _make_bands():
    out = np.zeros((128, NBC), np.float32)
    tp_ = np.arange(128)[:, None]
    t = np.arange(128)[None, :]
    for wi, w in enumerate(POOL_WINDOWS):
        inwin = (t - tp_ >= 0) & (t - tp_ < w)
        eye = (t == tp_).astype(np.float32)
        out[:, wi * 384:wi * 384 + 128] = inwin / float(w) - eye
        out[:, wi * 384 + 128:wi * 384 + 256] = ((128 + t - tp_) < w) / float(w)
        cnt = np.minimum(t + 1, w).astype(np.float32)
        out[:, wi * 384 + 256:wi * 384 + 384] = inwin / cnt - eye
        sb = 1536 + wi * 48
        for bl in range(8):
            for j in range(15):
                if j >= 16 - w:
                    out[bl * 15 + j, sb + bl] = 1.0 / w
                    out[bl * 15 + j, sb + 16 + 8 + bl] = 1.0 / w
        for r in range(16):
            out[r, sb + 32 + r] = 1.0 / w - 1.0
    return out


def kernel(x_prompt, x_sample, state_pool, state_conv, cache_mem_k, cache_mem_v, mem_prompt,
           g_mix, g_attn, g_mem, g_ffn, g_final, w_pool, pool_scale,
           w_conv_in, w_conv, w_conv_out, w_q, w_kv, w_o, w_up, w_down):
    f = lambda a: np.ascontiguousarray(np.asarray(a, dtype=np.float32))
    if "nc" not in _CACHE:
        _CACHE["nc"] = build_program()[0]
    nc = _CACHE["nc"]
    x_prompt, x_sample, state_pool, state_conv = f(x_prompt), f(x_sample), f(state_pool), f(state_conv)
    cache_mem_k, cache_mem_v, mem_prompt = f(cache_mem_k), f(cache_mem_v), f(mem_prompt)
    gv = f(np.concatenate([f(g_mix), f(g_attn), f(g_mem), f(g_ffn), f(g_final)[None, :], f(pool_scale), f(w_conv)[0]], axis=0))
    assert gv.shape == (NGV, D)
    shared = {
        "gv": gv, "ident": np.eye(128, dtype=np.float32), "bands": _make_bands(),
        "w_pool": f(w_pool)[0], "w_conv_in": f(w_conv_in)[0], "w_conv_out": f(w_conv_out)[0],
        "w_q": f(w_q), "w_kv": f(w_kv), "w_o": f(w_o), "w_up": f(w_up), "w_down": f(w_down),
    }
    in_maps = []
    for b in range(NCORES):
        s0, s1 = b * NS, (b + 1) * NS
        m = dict(shared)
        m["x"] = x_prompt[b]
        m["xs"] = f(x_sample[s0:s1, 0, :])
        m["spool"] = f(state_pool[0, s0:s1].reshape(NS * 15, D))
        m["sconv"] = f(state_conv[0, s0:s1].reshape(NS * 2, D))
        m["ck"] = f(cache_mem_k[:, s0:s1].reshape(2, NS, NMEM, D))
        m["cv"] = f(cache_mem_v[:, s0:s1].reshape(2, NS, NMEM, D))
        m["mem"] = mem_prompt[b]
        in_maps.append(m)
    res = run_bass_kernel_spmd(nc, in_maps, core_ids=list(range(NCORES)))
    rs = res.results
    g = lambda k: [np.asarray(r[k], dtype=np.float32) for r in rs]
    y_prompt = np.stack(g("y"), 0)
    y_sample = np.concatenate(g("ys"), 0).reshape(NCORES * NS, 1, D)
    new_pool_prompt = np.stack(g("npp"), 0)[None]
    new_conv_prompt = np.stack(g("ncp"), 0)[None]
    mem_k_prompt = np.stack(g("mk"), 1).reshape(2, NCORES, NMEM, 4, 256)
    mem_v_prompt = np.stack(g("mv"), 1).reshape(2, NCORES, NMEM, 4, 256)
    new_pool_sample = np.concatenate(g("nps"), 0)[None]
    new_conv_sample = np.concatenate(g("ncs"), 0)[None]
    return (y_prompt, y_sample, new_pool_prompt, new_conv_prompt, mem_k_prompt, mem_v_prompt,
            new_pool_sample, new_conv_sample)
```

```python
import contextlib
from bisect import bisect_right
import numpy as np
import concourse.bass as bass
import concourse.mybir as mybir
from concourse.bass_utils import run_bass_kernel_spmd

F32 = mybir.dt.float32
BF16 = mybir.dt.bfloat16
AF = mybir.ActivationFunctionType
ALU = mybir.AluOpType
AX = mybir.AxisListType

PE, ACT, DVE, POOL, SP = "pe", "act", "dve", "pool", "sp"
NCORES = 8
D = 1024
NCH = 8
SEQ = 2048
NS = 16
NTOK = SEQ + NS
NMEM = 256
EPS = 1e-6
PSB = 1 << 24
MEM_BYTES = 212480

TILES = [(0, 416, 416, 0), (416, 416, 416, 0), (832, 416, 416, 0), (1248, 416, 416, 0), (1664, 400, 384, 16)]
NT = len(TILES)
TW = 416

R_MIX, R_ATTN, R_MEM, R_FFN, R_FINAL, R_PSCALE, R_WCONV = 0, 2, 4, 6, 8, 9, 10
NGV = 13
NBC = 1536 + 4 * 48
POOL_WINDOWS = (2, 4, 8, 16)


class V:
    __slots__ = ("ap", "ivs", "excl")

    def __init__(self, ap, ivs, excl=False):
        self.ap, self.ivs, self.excl = ap, ivs, excl


class Op:
    __slots__ = ("eng", "kind", "fn", "chan", "idx", "sig", "val", "cwaits", "dwaits")


class Seg:
    __slots__ = ("w", "rc", "rd")

    def __init__(self):
        self.w, self.rc, self.rd = None, {}, []

    def copy(self):
        s = Seg()
        s.w, s.rc, s.rd = self.w, dict(self.rc), list(self.rd)
        return s


class Tracker:
    def __init__(self):
        self.bounds = [0, 1 << 40]
        self.segs = [Seg()]

    def _split(self, x):
        i = bisect_right(self.bounds, x) - 1
        if self.bounds[i] == x:
            return i
        self.bounds.insert(i + 1, x)
        self.segs.insert(i + 1, self.segs[i].copy())
        return i + 1

    def access(self, lo, hi, op, write, deps):
        i = self._split(lo)
        j = self._split(hi)
        for s in self.segs[i:j]:
            if s.w is not None:
                deps.add(s.w)
            if write:
                deps.update(s.rc.values())
                deps.update(s.rd)
                s.rc, s.rd, s.w = {}, [], op
            else:
                if op.kind == "d":
                    s.rd.append(op)
                else:
                    s.rc[op.eng] = op


class Prog:
    def __init__(self):
        self.q = {e: [] for e in (PE, ACT, DVE, POOL, SP)}
        self.tr = Tracker()
        self.chan_count = {}
        self.last = {}

    def add(self, eng, kind, fn, reads, writes, chan=None, after=()):
        op = Op()
        op.eng, op.kind, op.fn, op.chan = eng, kind, fn, chan
        op.idx = len(self.q[eng])
        op.sig, op.val = False, 0
        deps = set()
        for v in reads:
            for lo, hi in v.ivs:
                self.tr.access(lo, hi, op, v.excl, deps)
        for v in writes:
            for lo, hi in v.ivs:
                self.tr.access(lo, hi, op, True, deps)
        deps.update(after)
        deps.discard(op)
        cw, dw = {}, {}
        for d in deps:
            if d.kind == "d":
                dw[d.chan] = self.chan_count[d.chan] * 16
            else:
                if d.eng == eng and kind == "c" and eng == PE:
                    continue
                if d.eng not in cw or cw[d.eng].idx < d.idx:
                    cw[d.eng] = d
        for d in cw.values():
            d.sig = True
        op.cwaits = list(cw.values())
        op.dwaits = list(dw.items())
        if kind == "d":
            self.chan_count[chan] = self.chan_count.get(chan, 0) + 1
        self.q[eng].append(op)
        self.last[eng] = op
        return op


def _prod(xs):
    r = 1
    for x in xs:
        r *= x
    return r


def build_program():
    nc = bass.Bass("TRN2", target_bir_lowering=False)
    P = Prog()

    def din(name, shape):
        return nc.dram_tensor(name, list(shape), F32, kind="ExternalInput").ap()

    def dout(name, shape):
        return nc.dram_tensor(name, list(shape), F32, kind="ExternalOutput").ap()

    x_d = din("x", [SEQ, D])
    xs_d = din("xs", [NS, D])
    spool_d = din("spool", [NS * 15, D])
    sconv_d = din("sconv", [NS * 2, D])
    ck_d = din("ck", [2, NS, NMEM, D])
    cv_d = din("cv", [2, NS, NMEM, D])
    mem_d = din("mem", [NMEM, D])
    gv_d = din("gv", [NGV, D])
    ident_d = din("ident", [128, 128])
    bands_d = din("bands", [128, NBC])
    wpool_d = din("w_pool", [4, 256, 256])
    wcin_d = din("w_conv_in", [D, 3 * D])
    wcout_d = din("w_conv_out", [D, D])
    wq_d = din("w_q", [2, D, D])
    wkv_d = din("w_kv", [2, D, 2 * D])
    wo_d = din("w_o", [2, D, D])
    wup_d = din("w_up", [2, D, 4 * D])
    wdn_d = din("w_down", [2, 4 * D, D])

    y_d = dout("y", [SEQ, D])
    ys_d = dout("ys", [NS, D])
    npp_d = dout("npp", [15, D])
    ncp_d = dout("ncp", [2, D])
    mk_d = dout("mk", [2, NMEM, D])
    mv_d = dout("mv", [2, NMEM, D])
    nps_d = dout("nps", [NS, 15, D])
    ncs_d = dout("ncs", [NS, 2, D])

    es = contextlib.ExitStack()
    MEM = es.enter_context(nc.sbuf_tensor("MEM", [128, MEM_BYTES // 2], BF16))
    PS = es.enter_context(nc.psum_tensor("PS", [128, 8, 512], F32))

    def DV(ap):
        return V(ap, [])

    class T:
        def __init__(self, off, dt, shape):
            assert off % 4 == 0
            self.off, self.dt, self.shape = off, dt, tuple(shape)
            self.esz = 4 if dt == F32 else 2
            n = _prod(shape)
            self.nbytes = n * self.esz
            assert off + self.nbytes <= MEM_BYTES, (off, self.nbytes)
            base = MEM[:, off // 2: (off + self.nbytes) // 2]
            if dt == F32:
                base = base.bitcast(F32)
            if len(shape) == 2:
                base = base.rearrange("p (a b) -> p a b", a=shape[0])
            elif len(shape) == 3:
                base = base.rearrange("p (a b c) -> p a b c", a=shape[0], b=shape[1])
            self.ap = base
            st, s = [], 1
            for d in reversed(shape):
                st.append(s)
                s *= d
            self.strides = tuple(reversed(st))

        def __call__(self, *idx, p=None):
            idx = list(idx) + [slice(None)] * (len(self.shape) - len(idx))
            rng = []
            for i, d in zip(idx, self.shape):
                if isinstance(i, int):
                    rng.append((i, i + 1))
                else:
                    a = 0 if i.start is None else i.start
                    b = d if i.stop is None else i.stop
                    assert 0 <= a < b <= d, (idx, self.shape)
                    rng.append((a, b))
            key = (slice(None) if p is None else slice(p[0], p[1]),) + tuple(idx)
            ap = self.ap[key]
            k = len(self.shape)
            while k > 1 and rng[k - 1] == (0, self.shape[k - 1]):
                k -= 1
            ivs = []

            def rec(dim, base):
                if dim == k - 1:
                    a, b = rng[dim]
                    lo = base + a * self.strides[dim]
                    hi = base + b * self.strides[dim]
                    ivs.append((self.off + lo * self.esz, self.off + hi * self.esz))
                    return
                for i in range(rng[dim][0], rng[dim][1]):
                    rec(dim + 1, base + i * self.strides[dim])
            rec(0, 0)
            return V(ap, ivs)

    def sl(a, b):
        return slice(a, b)

    def PSv(bank, a, b, m=128):
        return V(PS[0:m, bank, a:b], [(PSB + bank * 2048, PSB + (bank + 1) * 2048)], excl=True)

    def PSr(bank, pat, m=128, **kw):
        return V(PS[0:m, bank, :].rearrange(pat, **kw), [(PSB + bank * 2048, PSB + (bank + 1) * 2048)], excl=True)

    bank_ctr = [0]

    bank_mod = [7]

    def nxt():
        b = bank_ctr[0] % bank_mod[0]
        bank_ctr[0] += 1
        return b

    top = [0]

    def palloc(dt, shape):
        t = T(top[0], dt, shape)
        top[0] += (t.nbytes + 63) // 64 * 64
        return t

    X = palloc(F32, [NCH, NTOK])
    U3 = palloc(BF16, [NCH, NTOK])
    WS = [palloc(BF16, [NCH, D]) for _ in range(3)]
    IDF = palloc(F32, [128])
    ONESB = palloc(BF16, [128])
    G = palloc(F32, [NCH, 16])
    INV = palloc(F32, [16])
    HALO = palloc(F32, [NCH, 2])
    PB = palloc(BF16, [NS, 2, 4])
    SS = palloc(F32, [NS, 2, 4])
    QS = palloc(BF16, [D])
    RDS = palloc(F32, [NS, 4])
    OTS = palloc(BF16, [NCH, NS])
    SCR0 = top[0]
    SCR_BYTES = MEM_BYTES - SCR0

    class Scratch:
        def __init__(self):
            self.top = SCR0

        def alloc(self, dt, shape):
            t = T(self.top, dt, shape)
            self.top += (t.nbytes + 63) // 64 * 64
            assert self.top <= MEM_BYTES, ("scratch overflow", self.top - SCR0, SCR_BYTES)
            return t

    def mm(out, lhsT, rhs, start, stop):
        P.add(PE, "c", lambda e, o=out.ap, l=lhsT.ap, r=rhs.ap: e.matmul(out=o, lhsT=l, rhs=r, start=start, stop=stop),
              [lhsT, rhs], [out])

    def mmx(out, lhsT, rhs, start, stop, is_transpose=False):
        P.add(PE, "c", lambda e, o=out.ap, l=lhsT.ap, r=rhs.ap: e.matmul(out=o, lhsT=l, rhs=r, start=start, stop=stop,
                                                                   is_transpose=is_transpose, skip_group_check=True),
              [lhsT, rhs], [out])

    def tp(out, in_, ident):
        P.add(PE, "c", lambda e, o=out.ap, i=in_.ap, d=ident.ap: e.transpose(out=o, in_=i, identity=d), [in_, ident], [out])

    def act(out, in_, func, scale=1.0, bias=0.0):
        rd = [in_]
        sc = scale
        if isinstance(scale, V):
            rd.append(scale)
            sc = scale.ap
        P.add(ACT, "c", lambda e, o=out.ap, i=in_.ap: e.activation(out=o, in_=i, func=func, bias=bias, scale=sc), rd, [out])

    def stt(out, in0, scalar, in1, op0, op1, eng=DVE):
        rd = [in0, in1]
        sc = scalar
        if isinstance(scalar, V):
            rd.append(scalar)
            sc = scalar.ap
        P.add(eng, "c", lambda e, o=out.ap, a=in0.ap, b=in1.ap: e.scalar_tensor_tensor(out=o, in0=a, scalar=sc, in1=b, op0=op0, op1=op1),
              rd, [out])

    def tt(out, in0, in1, op, eng=DVE):
        P.add(eng, "c", lambda e, o=out.ap, a=in0.ap, b=in1.ap: e.tensor_tensor(out=o, in0=a, in1=b, op=op), [in0, in1], [out])

    def tsc(out, in0, s1, op0, eng=DVE):
        rd = [in0]
        sc = s1
        if isinstance(s1, V):
            rd.append(s1)
            sc = s1.ap
        P.add(eng, "c", lambda e, o=out.ap, a=in0.ap: e.tensor_scalar(out=o, in0=a, scalar1=sc, scalar2=None, op0=op0), rd, [out])

    def cp(out, in_, eng=DVE):
        if eng == ACT:
            P.add(ACT, "c", lambda e, o=out.ap, i=in_.ap: e.copy(out=o, in_=i), [in_], [out])
        else:
            P.add(eng, "c", lambda e, o=out.ap, i=in_.ap: e.tensor_copy(out=o, in_=i), [in_], [out])

    def red(out, in_, eng=DVE):
        P.add(eng, "c", lambda e, o=out.ap, i=in_.ap: e.tensor_reduce(out=o, in_=i, axis=AX.X, op=ALU.add), [in_], [out])

    def memset(out, val, eng=DVE):
        P.add(eng, "c", lambda e, o=out.ap: e.memset(o, val), [], [out])

    def dma(q, out, in_, chan, after=()):
        P.add(q, "d", lambda e, o=out.ap, i=in_.ap: e.dma_start(out=o, in_=i), [in_], [out], chan=chan, after=after)

    evac_ctr = [0]

    def evac_eng():
        evac_ctr[0] += 1
        return ACT if evac_ctr[0] % 2 else DVE

    def wsrc_sq(ap2d):
        return ap2d.rearrange("(kc p) e -> p kc e", p=128)

    wlist = []
    for l in range(2):
        wlist.append((f"wk{l}", wsrc_sq(wkv_d[l][:, 0:D])))
        wlist.append((f"wv{l}", wsrc_sq(wkv_d[l][:, D:2 * D])))
        if l == 1:
            wlist.append(("cin_b", wsrc_sq(wcin_d[:, 0:D])))
            wlist.append(("cin_c", wsrc_sq(wcin_d[:, D:2 * D])))
            wlist.append(("cin_h", wsrc_sq(wcin_d[:, 2 * D:3 * D])))
            wlist.append(("cout", wsrc_sq(wcout_d)))
        wlist.append((f"wq{l}", wsrc_sq(wq_d[l])))
        wlist.append((f"wo{l}", wsrc_sq(wo_d[l])))
        for qq in range(4):
            wlist.append((f"up{l}_{qq}", wsrc_sq(wup_d[l][:, qq * D:(qq + 1) * D])))
            wlist.append((f"dn{l}_{qq}", wsrc_sq(wdn_d[l][qq * D:(qq + 1) * D, :])))
    widx = {n: i for i, (n, _) in enumerate(wlist)}
    wloaded = [0]

    def wload_next():
        i = wloaded[0]
        if i >= len(wlist):
            return
        wloaded[0] += 1
        dma(POOL, WS[i % 3](), DV(wlist[i][1]), f"w{i % 3}")

    def W(name):
        i = widx[name]
        assert i < wloaded[0], name
        return WS[i % 3]

    def wrelease(name):
        i = widx[name]
        while wloaded[0] <= i + 3 and wloaded[0] < len(wlist):
            wload_next()

    sc = Scratch()
    XS = [sc.alloc(F32, [D]) for _ in range(2)]
    GVS = sc.alloc(F32, [D])
    dma(SP, IDF(), DV(ident_d), "c")
    memset(ONESB(), 1.0)
    for j in range(15):
        memset(INV(sl(j, j + 1)), 1.0 / (j + 1))
    memset(HALO(), 0.0)

    def build_gains(GV2):
        bk = nxt()
        for c in range(NCH):
            tp(PSv(bk, c * 16, c * 16 + NGV), GV2(sl(c * 128, (c + 1) * 128), p=(0, NGV)), V(IDF.ap[0:NGV, 0:NGV], IDF().ivs))
        cp(G(sl(0, NCH), sl(0, NGV)), V(PS[:, bk, 0:128].rearrange("p (c r) -> p c r", c=NCH)[:, :, 0:NGV], PSv(bk, 0, 1).ivs, True))

    def mixer0_phase():
        bank_mod[0] = 8
        s2 = Scratch()
        s2.top = sc.top
        XS3 = [XS[0], XS[1], GVS, s2.alloc(F32, [D])]
        NXS = 4
        SQJ = s2.alloc(F32, [D])
        UTM = [s2.alloc(BF16, [D]) for _ in range(3)]
        GBC = s2.alloc(F32, [D])
        PLB = [s2.alloc(BF16, [NCH, 128]) for _ in range(2)]
        BANDS = s2.alloc(BF16, [NBC])
        WP = s2.alloc(BF16, [2, 4, 256])
        UF = s2.alloc(F32, [D])
        UCF = s2.alloc(F32, [D])
        WPF = T(UF.off, F32, [2, 4, 256])
        assert UCF.off == UF.off + 4096
        STAT = s2.alloc(F32, [4, 4])
        STB = [s2.alloc(BF16, [D]) for _ in range(2)]
        UCB = s2.alloc(BF16, [D])
        PLS = s2.alloc(BF16, [NCH, NS])
        SCBv = T(SQJ.off, F32, [4, 256])

        def band(wi, kind):
            a = wi * 384 + kind * 128
            return BANDS(sl(a, a + 128))

        GV2 = T(STB[0].off, F32, [D])
        assert STB[1].off == STB[0].off + 2048
        dma(SP, GBC(), DV(gv_d[R_MIX:R_MIX + 1, :].broadcast_to([128, D])), "c2")
        for tb0 in range(2):
            dma(SP, XS3[tb0](), DV(x_d[tb0 * 128:(tb0 + 1) * 128, :]), f"xs{tb0}")
        dma(SP, SQJ(), DV(gv_d[R_PSCALE:R_PSCALE + 1, :].broadcast_to([128, D])), "c2")
        dma(POOL, BANDS(), DV(bands_d), "wp")
        dma(SP, GV2(p=(0, NGV)), DV(gv_d), "c")
        for cc in range(2):
            dma(SP, WPF(cc), DV(wpool_d[:, cc * 128:(cc + 1) * 128, :].rearrange("g p e -> p g e")), "c2")
        for cc in range(2):
            tt(WP(cc), WPF(cc), SCBv(), ALU.mult)

        def stat(i, col):
            return STAT(i % 4, sl(col, col + 1))

        def stA(tb):
            xs = XS3[tb % NXS]
            if tb >= 2:
                dma(SP, xs(), DV(x_d[tb * 128:(tb + 1) * 128, :]), f"xs{tb % NXS}")
            act(SQJ(), xs(), AF.Square)
            red(stat(tb, 0), SQJ())
            act(stat(tb, 1), stat(tb, 0), AF.Ln, scale=1.0 / D, bias=EPS)
            act(stat(tb, 2), stat(tb, 1), AF.Exp, scale=-0.5)
            stt(UTM[tb % 3](), xs(), stat(tb, 2), GBC(), ALU.mult, ALU.mult)
            if tb == 15:
                stt(UF(), xs(), stat(tb, 2), GBC(), ALU.mult, ALU.mult)
                dma(SP, DV(npp_d), UF(p=(113, 128)), "uf")

        def stB(tb):
            bks = [nxt(), nxt()]
            utm = UTM[tb % 3]
            for c in range(NCH):
                wi = c // 2
                out = PSv(bks[c // 4], (c % 4) * 128, (c % 4 + 1) * 128)
                if tb == 0:
                    mm(out, utm(sl(c * 128, (c + 1) * 128)), band(wi, 2), True, True)
                else:
                    mm(out, UTM[(tb - 1) % 3](sl(c * 128, (c + 1) * 128)), band(wi, 1), True, False)
                    mm(out, utm(sl(c * 128, (c + 1) * 128)), band(wi, 0), False, True)
            for half in range(2):
                cp(PLB[tb % 2](sl(half * 4, half * 4 + 4)), PSr(bks[half], "p (j t) -> p j t", j=4), eng=ACT)

        def stC(tb):
            bks = [nxt(), nxt()]
            pl = PLB[tb % 2]
            xs = XS3[tb % NXS]
            for c in range(NCH):
                mmx(PSv(bks[c // 4], (c % 4) * 128, (c % 4 + 1) * 128), xs(sl(c * 128, (c + 1) * 128)), IDF(),
                    c % 4 == 0, False, is_transpose=True)
            for g in range(4):
                for j in range(2):
                    ec = 2 * g + j
                    out = PSv(bks[ec // 4], (ec % 4) * 128, (ec % 4 + 1) * 128)
                    for cc in range(2):
                        mmx(out, WP(cc, g, sl(j * 128, (j + 1) * 128)), pl(2 * g + cc), False, cc == 1)
            for half in range(2):
                cp(X(sl(half * 4, half * 4 + 4), sl(tb * 128, (tb + 1) * 128)),
                   PSr(bks[half], "p (j t) -> p j t", j=4), eng=(ACT if half == 0 else DVE))

        pieces = [(wi_, kc) for wi_ in range(3) for kc in range(NCH)]

        def wpiece():
            if not pieces:
                return
            wi_, kc = pieces.pop(0)
            dma(POOL, WS[wi_ % 3](kc), DV(wlist[wi_][1][:, kc, :]), f"w{wi_ % 3}", after=((P.last[DVE],) if wi_ < 2 else ()))

        for step in range(16 + 2):
            if step < 16:
                stA(step)
                if step == 2:
                    build_gains(GV2)
                if step >= 1:
                    wpiece()
                    wpiece()
            if 0 <= step - 1 < 16:
                stB(step - 1)
            if 0 <= step - 2 < 16:
                stC(step - 2)

        while pieces:
            wpiece()
        wloaded[0] = 3
        for i in range(2):
            dma(SP, XS3[i](p=(0, 120)), DV(spool_d[i * 120:(i + 1) * 120, :]), f"xs{i}")
            cp(STB[i](p=(0, 120)), XS3[i](p=(0, 120)), eng=(ACT if i == 0 else DVE))
        dma(SP, DV(nps_d[:, 0:14, :]), DV(spool_d.rearrange("(b j) d -> b j d", j=15)[:, 1:15, :]), "d2d")
        xs = XS3[2]
        dma(SP, xs(p=(0, NS)), DV(xs_d), "xs2")
        bk = nxt()
        for c in range(NCH):
            tp(PSv(bk, c * 16, c * 16 + 16), xs(sl(c * 128, (c + 1) * 128), p=(0, NS)), V(IDF.ap[0:NS, 0:NS], IDF().ivs))
        cp(X(sl(0, NCH), sl(SEQ, NTOK)), V(PS[:, bk, 0:128].rearrange("p (c r) -> p c r", c=NCH), PSv(bk, 0, 1).ivs, True))
        P16 = (0, NS)
        act(SQJ(p=P16), xs(p=P16), AF.Square)
        red(STAT(0, sl(0, 1), p=P16), SQJ(p=P16))
        act(STAT(0, sl(1, 2), p=P16), STAT(0, sl(0, 1), p=P16), AF.Ln, scale=1.0 / D, bias=EPS)
        act(STAT(0, sl(2, 3), p=P16), STAT(0, sl(1, 2), p=P16), AF.Exp, scale=-0.5)
        stt(UCF(p=P16), xs(p=P16), STAT(0, sl(2, 3), p=P16), GBC(p=P16), ALU.mult, ALU.mult)
        cp(UCB(p=P16), UCF(p=P16), eng=ACT)
        dma(SP, DV(nps_d[:, 14, :]), UCF(p=P16), "ucf")
        bk = nxt()
        for c in range(NCH):
            wi = c // 2
            sb = 1536 + wi * 48
            out = PSv(bk, c * 16, c * 16 + 16)
            mm(out, STB[0](sl(c * 128, (c + 1) * 128), p=(0, 120)), BANDS(sl(sb, sb + 16), p=(0, 120)), True, False)
            mm(out, STB[1](sl(c * 128, (c + 1) * 128), p=(0, 120)), BANDS(sl(sb + 16, sb + 32), p=(0, 120)), False, False)
            mm(out, UCB(sl(c * 128, (c + 1) * 128), p=P16), BANDS(sl(sb + 32, sb + 48), p=P16), False, True)
        cp(PLS(), V(PS[:, bk, 0:128].rearrange("p (c b) -> p c b", c=NCH), PSv(bk, 0, 1).ivs, True), eng=ACT)
        bk = nxt()
        for g in range(4):
            for j in range(2):
                ec = 2 * g + j
                for cc in range(2):
                    mm(PSv(bk, ec * 16, ec * 16 + 16), WP(cc, g, sl(j * 128, (j + 1) * 128)), PLS(2 * g + cc), cc == 0, cc == 1)
        xv = X(sl(0, NCH), sl(SEQ, NTOK))
        tt(xv, V(PS[:, bk, 0:128].rearrange("p (c b) -> p c b", c=NCH), PSv(bk, 0, 1).ivs, True), xv, ALU.add)

    def norm_a(c0, n, SQ):
        act(SQ(sl(0, NCH), sl(0, n)), X(sl(0, NCH), sl(c0, c0 + n)), AF.Square)

    def norm_b(c0, n, grow, outv, SQ, RSTD, extra=None):
        bk = nxt()
        for c in range(NCH):
            mm(PSv(bk, 0, n), ONESB(), SQ(c, sl(0, n)), c == 0, c == NCH - 1)
        act(PSv(bk, 0, n), PSv(bk, 0, n), AF.Ln, scale=1.0 / D, bias=EPS)
        act(RSTD(sl(0, n)), PSv(bk, 0, n), AF.Exp, scale=-0.5)
        for c in range(NCH):
            stt(outv(c), X(c, sl(c0, c0 + n)), G(c, sl(grow, grow + 1)), RSTD(sl(0, n)), ALU.mult, ALU.mult)
        if extra is not None:
            extra(RSTD)

    def norm(c0, n, grow, outv, SQ, RSTD, extra=None):
        norm_a(c0, n, SQ)
        norm_b(c0, n, grow, outv, SQ, RSTD, extra)

    def kv_phase(l, KT, VV):
        bank_mod[0] = 8
        o = U3.off
        ST = [T(o + i * 4096, F32, [D]) for i in range(3)]
        MEMT = T(o + 12288, F32, [NCH, NMEM])
        SQM = T(o + 20480, BF16, [NCH, NMEM])
        MEMN = T(o + 24576, BF16, [NCH, NMEM])
        RSM = T(o + 28672, F32, [NMEM])
        assert 28672 + 1024 <= U3.nbytes
        for nb in range(2):
            dma(SP, ST[nb](), DV(mem_d[nb * 128:(nb + 1) * 128, :]), f"st{nb}")
            for half in range(2):
                bk = nxt()
                for j in range(4):
                    c = half * 4 + j
                    tp(PSv(bk, j * 128, (j + 1) * 128), ST[nb](sl(c * 128, (c + 1) * 128)), IDF())
                cp(MEMT(sl(half * 4, half * 4 + 4), sl(nb * 128, (nb + 1) * 128)),
                   PSr(bk, "p (j t) -> p j t", j=4), eng=evac_eng())
        act(SQM(), MEMT(), AF.Square)
        bk = nxt()
        for c in range(NCH):
            mm(PSv(bk, 0, NMEM), ONESB(), SQM(c), c == 0, c == NCH - 1)
        act(PSv(bk, 0, NMEM), PSv(bk, 0, NMEM), AF.Ln, scale=1.0 / D, bias=EPS)
        act(RSM(), PSv(bk, 0, NMEM), AF.Exp, scale=-0.5)
        for c in range(NCH):
            stt(MEMN(c), MEMT(c), G(c, sl(R_MEM + l, R_MEM + l + 1)), RSM(), ALU.mult, ALU.mult)
        yield
        Wk, Wv = W(f"wk{l}"), W(f"wv{l}")
        for ep in range(4):
            bk = nxt()
            for j in range(2):
                ec = ep * 2 + j
                for kc in range(NCH):
                    mm(PSv(bk, j * 256, (j + 1) * 256), Wk(kc, sl(ec * 128, (ec + 1) * 128)), MEMN(kc), kc == 0, kc == NCH - 1)
            cp(KT(sl(ep * 2, ep * 2 + 2)), PSr(bk, "p (j n) -> p j n", j=2), eng=evac_eng())
            yield
        si = 0
        for which, Wt, od in ((0, Wk, mk_d), (1, Wv, mv_d)):
            for nb in range(2):
                st = ST[si % 3]
                for half in range(2):
                    bk = nxt()
                    for kc in range(NCH):
                        mm(PSv(bk, 0, 512), MEMN(kc, sl(nb * 128, (nb + 1) * 128)), Wt(kc, sl(half * 512, (half + 1) * 512)),
                           kc == 0, kc == NCH - 1)
                    cp(st(sl(half * 512, (half + 1) * 512)), PSv(bk, 0, 512), eng=ACT)
                    if which == 1:
                        cp(VV(nb, sl(half * 512, (half + 1) * 512)), st(sl(half * 512, (half + 1) * 512)), eng=DVE)
                    yield
                dma(SP, DV(od[l, nb * 128:(nb + 1) * 128, :]), st(), f"st{si % 3}")
                si += 1
        wrelease(f"wk{l}")
        wrelease(f"wv{l}")

    kv_scr_top = [SCR0]

    KT = palloc(BF16, [NCH, NMEM])
    VV = palloc(BF16, [2, D])
    SCRB = top[0]

    class Scratch2(Scratch):
        def __init__(self):
            self.top = SCRB
    kv_scr_top[0] = SCRB

    def pool_phase(kvgen):
        s2 = Scratch2()
        WP = s2.alloc(BF16, [2, 4, 256])
        SQ = s2.alloc(BF16, [NCH, TW])
        PL = s2.alloc(BF16, [NCH, TW])
        UBS = [s2.alloc(BF16, [NCH, 16 + TW]) for _ in range(2)]
        RSTDS = [s2.alloc(F32, [TW]) for _ in range(2)]
        U15 = s2.alloc(F32, [NCH, 16])
        FX = s2.alloc(F32, [16])
        SW = s2.alloc(F32, [NCH, NS])
        TA = [s2.alloc(F32, [16 + TW]) for _ in range(4)]
        TB = [s2.alloc(F32, [16 + TW]) for _ in range(4)]
        STG = [T(U3.off + i * 4096, F32, [D]) for i in range(2)]
        SPT = T(U3.off + 8192, F32, [NCH, NS, 16])
        for cc in range(2):
            dma(POOL, WP(cc), DV(wpool_d[:, cc * 128:(cc + 1) * 128, :].rearrange("g p e -> p g e")), "wp")
        memset(UBS[0](sl(0, NCH), sl(0, 16)), 0.0)
        dma(SP, STG[0](), DV(spool_d[0:128, :]), "stg0")
        dma(SP, STG[1](p=(0, 112)), DV(spool_d[128:240, :]), "stg1")
        for cpair in range(4):
            bk = nxt()
            for j in range(2):
                c = cpair * 2 + j
                tp(PSv(bk, j * 256, j * 256 + 128), STG[0](sl(c * 128, (c + 1) * 128)), IDF())
                tp(PSv(bk, j * 256 + 128, j * 256 + 240), STG[1](sl(c * 128, (c + 1) * 128), p=(0, 112)),
                   V(IDF.ap[0:112, 0:112], IDF().ivs))
            for j in range(2):
                c = cpair * 2 + j
                cp(SPT(c, sl(0, NS), sl(0, 15)),
                   V(PS[:, bk, j * 256:j * 256 + 240].rearrange("p (b j) -> p b j", b=NS), PSv(bk, 0, 1).ivs, True),
                   eng=evac_eng())
        dma(SP, DV(nps_d[:, 0:14, :]), DV(spool_d.rearrange("(b j) d -> b j d", j=15)[:, 1:15, :]), "d2d")

        def do_norm(ti):
            c0, n, npmt, ns = TILES[ti]
            last = ti == NT - 1
            UBt = UBS[ti % 2]

            def extra(RS, c0=c0, n=n, npmt=npmt):
                for c in range(NCH):
                    stt(SPT(c, sl(0, NS), 15), X(c, sl(c0 + npmt, c0 + n)), G(c, sl(R_MIX, R_MIX + 1)),
                        RS(sl(npmt, n)), ALU.mult, ALU.mult)
                    stt(U15(c, sl(0, 15)), X(c, sl(c0 + npmt - 15, c0 + npmt)), G(c, sl(R_MIX, R_MIX + 1)),
                        RS(sl(npmt - 15, npmt)), ALU.mult, ALU.mult)
            if ti > 0:
                pn = TILES[ti - 1][2]
                cp(UBt(sl(0, NCH), sl(0, 16)), UBS[(ti - 1) % 2](sl(0, NCH), sl(pn, pn + 16)))
            norm(c0, n, R_MIX + 0, lambda c, n=n: UBt(c, sl(16, 16 + n)), SQ, RSTDS[ti % 2], extra if last else None)

        do_norm(0)
        for ti in range(NT):
            c0, n, npmt, ns = TILES[ti]
            last = ti == NT - 1
            E = 16 + npmt
            UB = UBS[ti % 2]
            if not last:
                do_norm(ti + 1)
            for c in range(NCH):
                g = c // 2
                L = g + 1
                w = 1 << L
                we = POOL if c >= 4 else DVE
                pi = 0 if c < 4 else 1 + (c - 4) % 3
                ta, tb = TA[pi], TB[pi]
                tt(ta(sl(2, E)), UB(c, sl(2, E)), UB(c, sl(1, E - 1)), ALU.add, eng=we)
                if L >= 2:
                    tt(tb(sl(4, E)), ta(sl(4, E)), ta(sl(2, E - 2)), ALU.add, eng=we)
                if L >= 3:
                    tt(ta(sl(8, E)), tb(sl(8, E)), tb(sl(4, E - 4)), ALU.add, eng=we)
                if L >= 4:
                    tt(tb(sl(16, E)), ta(sl(16, E)), ta(sl(8, E - 8)), ALU.add, eng=we)
                al = ta if L in (1, 3) else tb
                stt(PL(c, sl(0, npmt)), al(sl(16, E)), 1.0 / w, UB(c, sl(16, E)), ALU.mult, ALU.subtract)
                if ti == 0:
                    tt(FX(sl(0, w - 1)), al(sl(16, 16 + w - 1)), INV(sl(0, w - 1)), ALU.mult)
                    tt(PL(c, sl(0, w - 1)), FX(sl(0, w - 1)), UB(c, sl(16, 16 + w - 1)), ALU.subtract)
            if last:
                for g in range(4):
                    w = 1 << (g + 1)
                    red(SW(sl(2 * g, 2 * g + 2), sl(0, NS)), SPT(sl(2 * g, 2 * g + 2), sl(0, NS), sl(16 - w, 16)))
                    stt(PL(sl(2 * g, 2 * g + 2), sl(npmt, n)), SW(sl(2 * g, 2 * g + 2), sl(0, NS)), 1.0 / w,
                        SPT(sl(2 * g, 2 * g + 2), sl(0, NS), 15), ALU.mult, ALU.subtract)
            for g in range(4):
                for j in range(2):
                    ec = 2 * g + j
                    bk = nxt()
                    for cc in range(2):
                        mm(PSv(bk, 0, n), WP(cc, g, sl(j * 128, (j + 1) * 128)), PL(2 * g + cc, sl(0, n)), cc == 0, cc == 1)
                    stt(X(ec, sl(c0, c0 + n)), PSv(bk, 0, n), G(ec, sl(R_PSCALE, R_PSCALE + 1)), X(ec, sl(c0, c0 + n)),
                        ALU.mult, ALU.add)
            for _ in range(3):
                next(kvgen, None)
        for _ in kvgen:
            pass
        bks = [nxt(), nxt()]
        for c in range(NCH):
            tp(PSv(bks[c // 4], (c % 4) * 128, (c % 4 + 1) * 128, m=15), U15(c, sl(0, 15)), IDF())
        for hf in range(2):
            cp(STG[0](sl(hf * 512, (hf + 1) * 512), p=(0, 15)), PSv(bks[hf], 0, 512, m=15), eng=evac_eng())
        dma(SP, DV(npp_d), STG[0](p=(0, 15)), "stg0")
        bks = [nxt(), nxt()]
        for c in range(NCH):
            tp(PSv(bks[c // 4], (c % 4) * 128, (c % 4 + 1) * 128, m=NS), SPT(c, sl(0, NS), 15), IDF())
        for hf in range(2):
            cp(STG[1](sl(hf * 512, (hf + 1) * 512), p=(0, NS)), PSv(bks[hf], 0, 512, m=NS), eng=evac_eng())
        dma(SP, DV(nps_d[:, 14, :]), STG[1](p=(0, NS)), "stg1")

    def conv_phase():
        bank_mod[0] = 8
        s2 = Scratch2()
        SQ = s2.alloc(BF16, [NCH, TW])
        RSTD = [s2.alloc(F32, [TW]) for _ in range(2)]
        UT = [s2.alloc(BF16, [NCH, TW]) for _ in range(2)]
        GC = [s2.alloc(F32, [TW]) for _ in range(2)]
        ACC = [s2.alloc(F32, [TW]) for _ in range(2)]
        CH = [s2.alloc(F32, [2 + TW]) for _ in range(2)]
        SCT = s2.alloc(F32, [NCH, NS, 2])
        NCt = s2.alloc(F32, [NCH, 2])
        CHS = s2.alloc(F32, [NCH, NS])
        STG = [s2.alloc(F32, [D]) for _ in range(2)]
        Wb, Wc, Wh = W("cin_b"), W("cin_c"), W("cin_h")
        dma(SP, STG[0](p=(0, 32)), DV(sconv_d), "stg0")
        bk = nxt()
        for c in range(NCH):
            tp(PSv(bk, c * 32, (c + 1) * 32), STG[0](sl(c * 128, (c + 1) * 128), p=(0, 32)), V(IDF.ap[0:32, 0:32], IDF().ivs))
        cp(SCT(), V(PS[:, bk, 0:256].rearrange("p (c b j) -> p c b j", c=NCH, b=NS), PSv(bk, 0, 1).ivs, True))
        dma(SP, DV(ncs_d[:, 0, :]), DV(sconv_d.rearrange("(b j) d -> b j d", j=2)[:, 1, :]), "d2d")

        def nA(ti):
            c0, n, npmt, ns = TILES[ti]
            norm_a(c0, n, SQ)

        def nB(ti):
            c0, n, npmt, ns = TILES[ti]
            norm_b(c0, n, R_MIX + 1, lambda c, n=n, ti=ti: UT[ti % 2](c, sl(0, n)), SQ, RSTD[ti % 2])

        def stB(ti, mid=None):
            c0, n, npmt, ns = TILES[ti]
            last = ti == NT - 1
            ut = UT[ti % 2]
            for c in range(NCH):
                if c == 4 and mid is not None:
                    mid()
                k = c % 2
                bb, bc_, bh = nxt(), nxt(), nxt()
                for bkx, Wt in ((bc_, Wc), (bh, Wh), (bb, Wb)):
                    for kc in range(NCH):
                        mm(PSv(bkx, 0, n), Wt(kc, sl(c * 128, (c + 1) * 128)), ut(kc, sl(0, n)), kc == 0, kc == NCH - 1)
                cp(GC[k](sl(0, n)), PSv(bc_, 0, n), eng=ACT)
                cp(CH[k](sl(0, 2)), HALO(c))
                tt(CH[k](sl(2, 2 + n)), PSv(bh, 0, n), GC[k](sl(0, n)), ALU.mult)
                act(ACC[k](sl(0, n)), CH[k](sl(2, 2 + n)), AF.Copy, scale=G(c, sl(R_WCONV + 2, R_WCONV + 3)))
                stt(ACC[k](sl(0, npmt)), CH[k](sl(1, 1 + npmt)), G(c, sl(R_WCONV + 1, R_WCONV + 2)), ACC[k](sl(0, npmt)),
                    ALU.mult, ALU.add)
                stt(ACC[k](sl(0, npmt)), CH[k](sl(0, npmt)), G(c, sl(R_WCONV + 0, R_WCONV + 1)), ACC[k](sl(0, npmt)),
                    ALU.mult, ALU.add)
                if last:
                    stt(ACC[k](sl(npmt, n)), SCT(c, sl(0, NS), 1), G(c, sl(R_WCONV + 1, R_WCONV + 2)), ACC[k](sl(npmt, n)),
                        ALU.mult, ALU.add)
                    stt(ACC[k](sl(npmt, n)), SCT(c, sl(0, NS), 0), G(c, sl(R_WCONV + 0, R_WCONV + 1)), ACC[k](sl(npmt, n)),
                        ALU.mult, ALU.add)
                    cp(NCt(c), CH[k](sl(npmt, npmt + 2)))
                    cp(CHS(c), CH[k](sl(2 + npmt, 2 + n)))
                else:
                    cp(HALO(c), CH[k](sl(npmt, npmt + 2)))
                tt(U3(c, sl(c0, c0 + n)), PSv(bb, 0, n), ACC[k](sl(0, n)), ALU.mult)

        nA(0)
        nB(0)
        nA(1)
        for ti in range(NT):
            if ti + 1 < NT:
                nB(ti + 1)
            stB(ti, (lambda ti=ti: nA(ti + 2)) if ti + 2 < NT else None)
        wrelease("cin_b")
        wrelease("cin_c")
        wrelease("cin_h")
        bks = [nxt(), nxt()]
        for c in range(NCH):
            tp(PSv(bks[c // 4], (c % 4) * 128, (c % 4 + 1) * 128, m=2), NCt(c), IDF())
        for hf in range(2):
            cp(STG[0](sl(hf * 512, (hf + 1) * 512), p=(0, 2)), PSv(bks[hf], 0, 512, m=2), eng=evac_eng())
        dma(SP, DV(ncp_d), STG[0](p=(0, 2)), "stg0")
        bks = [nxt(), nxt()]
        for c in range(NCH):
            tp(PSv(bks[c // 4], (c % 4) * 128, (c % 4 + 1) * 128, m=NS), CHS(c), IDF())
        for hf in range(2):
            cp(STG[1](sl(hf * 512, (hf + 1) * 512), p=(0, NS)), PSv(bks[hf], 0, 512, m=NS), eng=evac_eng())
        dma(SP, DV(ncs_d[:, 1, :]), STG[1](p=(0, NS)), "stg1")
        Wco = W("cout")
        for ti in range(NT):
            c0, n, npmt, ns = TILES[ti]
            for ec in range(NCH):
                bk = nxt()
                for kc in range(NCH):
                    mm(PSv(bk, 0, n), Wco(kc, sl(ec * 128, (ec + 1) * 128)), U3(kc, sl(c0, c0 + n)), kc == 0, kc == NCH - 1)
                tt(X(ec, sl(c0, c0 + n)), PSv(bk, 0, n), X(ec, sl(c0, c0 + n)), ALU.add)
        wrelease("cout")

    def attn_phase(l):
        bank_mod[0] = 7
        s2 = Scratch2()
        SQ = s2.alloc(BF16, [NCH, TW])
        QT = s2.alloc(BF16, [NCH, TW])
        BT = [s2.alloc(BF16, [NCH, TW]) for _ in range(2)]
        PT = [s2.alloc(BF16, [2, TW]) for _ in range(2)]
        RSTD = [s2.alloc(F32, [TW]) for _ in range(2)]
        RD = [s2.alloc(F32, [TW]) for _ in range(2)]
        KS = [T(U3.off + i * 4096, F32, [D]) for i in range(4)]
        VS = [T(U3.off + 16384 + i * 2048, BF16, [2, 512]) for i in range(4)]
        PRf = [T(U3.off + 24576 + i * 4096, F32, [D]) for i in range(2)]
        PR = [T(U3.off + 24576 + i * 4096, F32, [4, 256]) for i in range(2)]
        assert 24576 + 8192 <= U3.nbytes
        QBC = [s2.alloc(BF16, [D]) for _ in range(2)]
        SEL = s2.alloc(BF16, [NS, 128])
        RDSf = T(RDS.off, F32, [NS * 4])
        Wq, Wo = W(f"wq{l}"), W(f"wo{l}")
        cp(SEL(p=(0, NS)), V(IDF.ap[0:NS, 0:NS].unsqueeze(2).broadcast_to([NS, NS, 128]), IDF().ivs))
        seq = [4, 0, 1, 2, 3]

        def N2(k):
            N2a(k)
            N2b(k)

        def N2a(k):
            c0, n, npmt, ns = TILES[seq[k]]
            norm_a(c0, n, SQ)

        def N2b(k):
            c0, n, npmt, ns = TILES[seq[k]]
            norm_b(c0, n, R_ATTN + l, lambda c, n=n, k=k: BT[k % 2](c, sl(0, n)), SQ, RSTD[k % 2])

        def Q(k, mid=None):
            ti = seq[k]
            c0, n, npmt, ns = TILES[ti]
            ut = BT[k % 2]
            for ec in range(NCH):
                bk = nxt()
                for kc in range(NCH):
                    mm(PSv(bk, 0, n), Wq(kc, sl(ec * 128, (ec + 1) * 128)), ut(kc, sl(0, n)), kc == 0, kc == NCH - 1)
                cp(QT(ec, sl(0, n)), PSv(bk, 0, n), eng=ACT)
                if ec == 3 and mid is not None:
                    mid()
            if ns:
                for hf in range(2):
                    bk = nxt()
                    for kc in range(NCH):
                        mm(PSv(bk, 0, 512, m=NS), ut(kc, sl(npmt, n)), Wq(kc, sl(hf * 512, (hf + 1) * 512)), kc == 0, kc == NCH - 1)
                    cp(QS(sl(hf * 512, (hf + 1) * 512), p=(0, NS)), PSv(bk, 0, 512, m=NS), eng=ACT)

        def ATT(k, slot=None):
            ti = seq[k]
            c0, n, npmt, ns = TILES[ti]
            ot = BT[k % 2]

            def S(h):
                pt = PT[h % 2]
                for ncx in range(2):
                    bk = nxt()
                    for j in range(2):
                        mm(PSv(bk, 0, npmt), KT(2 * h + j, sl(ncx * 128, (ncx + 1) * 128)), QT(2 * h + j, sl(0, npmt)), j == 0, j == 1)
                    act(pt(ncx, sl(0, npmt)), PSv(bk, 0, npmt), AF.Exp, scale=1.0 / 16.0)

            def PVh(h):
                pt = PT[h % 2]
                bo = [nxt(), nxt()]
                bd = nxt()
                for ncx in range(2):
                    mm(PSv(bd, 0, npmt), ONESB(), pt(ncx, sl(0, npmt)), ncx == 0, ncx == 1)
                for j in range(2):
                    for ncx in range(2):
                        mm(PSv(bo[j], 0, npmt), VV(ncx, sl((2 * h + j) * 128, (2 * h + j + 1) * 128)), pt(ncx, sl(0, npmt)),
                           ncx == 0, ncx == 1)
                act(PSv(bd, 0, npmt), PSv(bd, 0, npmt), AF.Ln)
                act(RD[h % 2](sl(0, npmt)), PSv(bd, 0, npmt), AF.Exp, scale=-1.0)
                for j in range(2):
                    tt(ot(2 * h + j, sl(0, npmt)), PSv(bo[j], 0, npmt), RD[h % 2](sl(0, npmt)), ALU.mult)

            S(0)
            for h in range(4):
                if h + 1 < 4:
                    S(h + 1)
                PVh(h)
                if slot is not None:
                    slot(h)

        def O(k, mid=None):
            ti = seq[k]
            c0, n, npmt, ns = TILES[ti]
            ot = BT[k % 2]
            for ec in range(NCH):
                bk = nxt()
                for kc in range(NCH):
                    mm(PSv(bk, 0, npmt), Wo(kc, sl(ec * 128, (ec + 1) * 128)), ot(kc, sl(0, npmt)), kc == 0, kc == NCH - 1)
                tt(X(ec, sl(c0, c0 + npmt)), PSv(bk, 0, npmt), X(ec, sl(c0, c0 + npmt)), ALU.add)
                if ec == 3 and mid is not None:
                    mid()

        def SD(b):
            for ncx in range(2):
                i = (2 * b + ncx) % 4
                dma(SP, KS[i](), DV(ck_d[l, b, ncx * 128:(ncx + 1) * 128, :]), f"ks{i}")
            for hf in range(2):
                i = (2 * b + hf) % 4
                dma(POOL, VS[i](), DV(cv_d[l, b].rearrange("(nc p) e -> p nc e", p=128)[:, :, hf * 512:(hf + 1) * 512]), f"vs{i}")

        def SA(b):
            qb = QBC[b % 2]
            for h2 in range(2):
                bk = nxt()
                mm(PSv(bk, 0, 512), SEL(b, p=(0, NS)), QS(sl(h2 * 512, (h2 + 1) * 512), p=(0, NS)), True, True)
                cp(qb(sl(h2 * 512, (h2 + 1) * 512)), PSv(bk, 0, 512), eng=ACT)
            for ncx in range(2):
                i = (2 * b + ncx) % 4
                tt(PRf[ncx](), KS[i](), qb(), ALU.mult)
                red(SS(b, ncx), PR[ncx]())

        def SE(b):
            act(PB(b), SS(b), AF.Exp, scale=1.0 / 16.0)

        def SB(b):
            for ncx in range(2):
                mm(PSv(7, 128 + b * 4, 128 + b * 4 + 4), ONESB(), PB(b, ncx), ncx == 0, ncx == 1)
            for hf in range(2):
                i = (2 * b + hf) % 4
                for cc in range(4):
                    c = hf * 4 + cc
                    h = c // 2
                    for ncx in range(2):
                        mm(PSv(7, c * 16 + b, c * 16 + b + 1), VS[i](ncx, sl(cc * 128, (cc + 1) * 128)), PB(b, ncx, sl(h, h + 1)),
                           ncx == 0, ncx == 1)

        N2(0)
        Q(0)
        SD(0)
        sctr = [0]

        def sample_slot():
            b = sctr[0]
            if 0 < b <= NS:
                SE(b - 1)
                SB(b - 1)
            if b + 1 < NS:
                SD(b + 1)
            if b < NS:
                SA(b)
            sctr[0] += 1

        def make_slot(k):
            def slot(h):
                if h == 0 and k + 1 < 5:
                    N2a(k + 1)
                if h == 1 and k + 1 < 5:
                    N2b(k + 1)
                if h == 1 or h == 3:
                    sample_slot()
            return slot

        for k in range(5):
            ATT(k, make_slot(k))
            O(k, sample_slot)
            if k + 1 < 5:
                Q(k + 1, sample_slot)
        while sctr[0] <= NS:
            sample_slot()
        assert sctr[0] > NS
        wrelease(f"wq{l}")

        def finish():
            act(PSv(7, 128, 192), PSv(7, 128, 192), AF.Ln)
            act(RDSf(), PSv(7, 128, 192), AF.Exp, scale=-1.0)
            for c in range(NCH):
                tt(OTS(c), PSv(7, c * 16, c * 16 + 16), RDS(sl(0, NS), c // 2), ALU.mult)
            for ec in range(NCH):
                bk = nxt()
                for kc in range(NCH):
                    mm(PSv(bk, 0, NS), Wo(kc, sl(ec * 128, (ec + 1) * 128)), OTS(kc), kc == 0, kc == NCH - 1)
                tt(X(ec, sl(SEQ, NTOK)), PSv(bk, 0, NS), X(ec, sl(SEQ, NTOK)), ALU.add)
            wrelease(f"wo{l}")
        return finish

    def mlp_phase(l, finish_attn):
        bank_mod[0] = 7
        fin = l == 1
        s2 = Scratch() if fin else Scratch2()
        SQ = s2.alloc(BF16, [NCH, TW])
        RSTD = [s2.alloc(F32, [TW]) for _ in range(2)]
        H = [s2.alloc(BF16, [NCH, TW]) for _ in range(2)]
        R = [s2.alloc(F32, [TW]) for _ in range(4)]
        if fin:
            STG = [s2.alloc(F32, [D]) for _ in range(2)]
            GB = s2.alloc(F32, [D])
            SQT = s2.alloc(F32, [D])
            SSs = s2.alloc(F32, [4])
            dma(SP, GB(), DV(gv_d[R_FINAL:R_FINAL + 1, :].broadcast_to([128, D])), "gb")
        rc = [0]
        stg_ctr = [0]

        def N3(ti):
            c0, n, npmt, ns = TILES[ti]
            if ti == NT - 1:
                finish_attn()
            norm(c0, n, R_FFN + l, lambda c, c0=c0, n=n: U3(c, sl(c0, c0 + n)), SQ, RSTD[ti % 2])

        def stA(q, ti, j):
            c0, n, npmt, ns = TILES[ti]
            Wu = W(f"up{l}_{q}")
            h = H[j % 2]
            for fc in range(NCH):
                bk = nxt()
                for kc in range(NCH):
                    mm(PSv(bk, 0, n), Wu(kc, sl(fc * 128, (fc + 1) * 128)), U3(kc, sl(c0, c0 + n)), kc == 0, kc == NCH - 1)
                r = R[rc[0] % 4]
                rc[0] += 1
                act(r(sl(0, n)), PSv(bk, 0, n), AF.Relu)
                if fin and q == 3:
                    act(h(fc, sl(0, n)), r(sl(0, n)), AF.Square)
                else:
                    tt(h(fc, sl(0, n)), r(sl(0, n)), r(sl(0, n)), ALU.mult)
            if ti == NT - 1:
                wrelease(f"up{l}_{q}")

        def stB(q, ti, j):
            c0, n, npmt, ns = TILES[ti]
            Wd = W(f"dn{l}_{q}")
            h = H[j % 2]
            for ec in range(NCH):
                bk = nxt()
                for fc in range(NCH):
                    mm(PSv(bk, 0, n), Wd(fc, sl(ec * 128, (ec + 1) * 128)), h(fc, sl(0, n)), fc == 0, fc == NCH - 1)
                tt(X(ec, sl(c0, c0 + n)), PSv(bk, 0, n), X(ec, sl(c0, c0 + n)), ALU.add)
            if ti == NT - 1:
                wrelease(f"dn{l}_{q}")

        def final_tile(ti):
            c0, n, npmt, ns = TILES[ti]
            blocks = [(a, min(npmt, a + 128), False) for a in range(0, npmt, 128)]
            if ns:
                blocks.append((npmt, n, True))
            for a, b, is_s in blocks:
                m = b - a
                bks = [nxt(), nxt()]
                for c in range(NCH):
                    tp(PSv(bks[c // 4], (c % 4) * 128, (c % 4 + 1) * 128, m=m), X(c, sl(c0 + a, c0 + b)), IDF())
                k = stg_ctr[0] % 2
                stg_ctr[0] += 1
                st = STG[k]
                for hf in range(2):
                    act(SQT(sl(hf * 512, (hf + 1) * 512), p=(0, m)), PSv(bks[hf], 0, 512, m=m), AF.Square)
                red(SSs(sl(0, 1), p=(0, m)), SQT(p=(0, m)))
                act(SSs(sl(1, 2), p=(0, m)), SSs(sl(0, 1), p=(0, m)), AF.Ln, scale=1.0 / D, bias=EPS)
                act(SSs(sl(2, 3), p=(0, m)), SSs(sl(1, 2), p=(0, m)), AF.Exp, scale=-0.5)
                for hf in range(2):
                    stt(st(sl(hf * 512, (hf + 1) * 512), p=(0, m)), PSv(bks[hf], 0, 512, m=m), SSs(sl(2, 3), p=(0, m)),
                        GB(sl(hf * 512, (hf + 1) * 512), p=(0, m)), ALU.mult, ALU.mult)
                if is_s:
                    dma(SP, DV(ys_d), st(p=(0, m)), f"stg{k}")
                else:
                    dma(SP, DV(y_d[c0 + a:c0 + b, :]), st(p=(0, m)), f"stg{k}")

        items = [(q, ti) for q in range(4) for ti in range(NT)]
        N3(0)
        N3(1)
        stA(0, 0, 0)
        pend = []
        for j, (q, ti) in enumerate(items):
            if q == 0 and ti + 2 < NT:
                N3(ti + 2)
            if j + 1 < len(items):
                stA(items[j + 1][0], items[j + 1][1], j + 1)
            if fin and pend:
                final_tile(pend.pop(0))
            stB(q, ti, j)
            if fin and q == 3:
                pend.append(ti)
                if ti == NT - 1:
                    while pend:
                        final_tile(pend.pop(0))
        while fin and pend:
            final_tile(pend.pop(0))

    mixer0_phase()
    for _ in kv_phase(0, KT, VV):
        pass
    fin = attn_phase(0)
    mlp_phase(0, fin)
    for _ in kv_phase(1, KT, VV):
        pass
    conv_phase()
    fin = attn_phase(1)
    mlp_phase(1, fin)

    sems = {e: es.enter_context(nc.semaphore(f"s_{e}")) for e in (PE, ACT, DVE, POOL)}
    chans = {c: es.enter_context(nc.semaphore(f"c_{c}")) for c in P.chan_count}
    for e in (PE, ACT, DVE, POOL, SP):
        cnt = 0
        for op in P.q[e]:
            if op.kind == "c" and op.sig:
                cnt += 1
            op.val = cnt

    def emit(name, e):
        known = {}
        for op in P.q[name]:
            for d in op.cwaits:
                key = ("e", d.eng)
                if known.get(key, 0) < d.val:
                    e.wait_ge(sems[d.eng], d.val)
                    known[key] = d.val
            for ch, val in op.dwaits:
                key = ("c", ch)
                if known.get(key, 0) < val:
                    e.wait_ge(chans[ch], val)
                    known[key] = val
            ins = op.fn(e)
            if op.kind == "d":
                ins.then_inc(chans[op.chan], 16)
            elif op.sig:
                ins.then_inc(sems[name], 1)
        if name == SP:
            for ch, cnt in P.chan_count.items():
                if known.get(("c", ch), 0) < cnt * 16:
                    e.wait_ge(chans[ch], cnt * 16)

    with nc.Block() as block:
        @block.tensor
        def _(e):
            emit(PE, e)

        @block.scalar
        def _(e):
            emit(ACT, e)

        @block.vector
        def _(e):
            emit(DVE, e)

        @block.gpsimd
        def _(e):
            emit(POOL, e)

        @block.sync
        def _(e):
            emit(SP, e)
    es.close()
    stats = {k: len(v) for k, v in P.q.items()}
    return nc, stats


_CACHE = {}


def _make_bands():
    out = np.zeros((128, NBC), np.float32)
    tp_ = np.arange(128)[:, None]
    t = np.arange(128)[None, :]
    for wi, w in enumerate(POOL_WINDOWS):
        inwin = (t - tp_ >= 0) & (t - tp_ < w)
        eye = (t == tp_).astype(np.float32)
        out[:, wi * 384:wi * 384 + 128] = inwin / float(w) - eye
        out[:, wi * 384 + 128:wi * 384 + 256] = ((128 + t - tp_) < w) / float(w)
        cnt = np.minimum(t + 1, w).astype(np.float32)
        out[:, wi * 384 + 256:wi * 384 + 384] = inwin / cnt - eye
        sb = 1536 + wi * 48
        for bl in range(8):
            for j in range(15):
                if j >= 16 - w:
                    out[bl * 15 + j, sb + bl] = 1.0 / w
                    out[bl * 15 + j, sb + 16 + 8 + bl] = 1.0 / w
        for r in range(16):
            out[r, sb + 32 + r] = 1.0 / w - 1.0
    return out


def kernel(x_prompt, x_sample, state_pool, state_conv, cache_mem_k, cache_mem_v, mem_prompt,
           g_mix, g_attn, g_mem, g_ffn, g_final, w_pool, pool_scale,
           w_conv_in, w_conv, w_conv_out, w_q, w_kv, w_o, w_up, w_down):
    f = lambda a: np.ascontiguousarray(np.asarray(a, dtype=np.float32))
    if "nc" not in _CACHE:
        _CACHE["nc"] = build_program()[0]
    nc = _CACHE["nc"]
    x_prompt, x_sample, state_pool, state_conv = f(x_prompt), f(x_sample), f(state_pool), f(state_conv)
    cache_mem_k, cache_mem_v, mem_prompt = f(cache_mem_k), f(cache_mem_v), f(mem_prompt)
    gv = f(np.concatenate([f(g_mix), f(g_attn), f(g_mem), f(g_ffn), f(g_final)[None, :], f(pool_scale), f(w_conv)[0]], axis=0))
    assert gv.shape == (NGV, D)
    shared = {
        "gv": gv, "ident": np.eye(128, dtype=np.float32), "bands": _make_bands(),
        "w_pool": f(w_pool)[0], "w_conv_in": f(w_conv_in)[0], "w_conv_out": f(w_conv_out)[0],
        "w_q": f(w_q), "w_kv": f(w_kv), "w_o": f(w_o), "w_up": f(w_up), "w_down": f(w_down),
    }
    in_maps = []
    for b in range(NCORES):
        s0, s1 = b * NS, (b + 1) * NS
        m = dict(shared)
        m["x"] = x_prompt[b]
        m["xs"] = f(x_sample[s0:s1, 0, :])
        m["spool"] = f(state_pool[0, s0:s1].reshape(NS * 15, D))
        m["sconv"] = f(state_conv[0, s0:s1].reshape(NS * 2, D))
        m["ck"] = f(cache_mem_k[:, s0:s1].reshape(2, NS, NMEM, D))
        m["cv"] = f(cache_mem_v[:, s0:s1].reshape(2, NS, NMEM, D))
        m["mem"] = mem_prompt[b]
        in_maps.append(m)
    res = run_bass_kernel_spmd(nc, in_maps, core_ids=list(range(NCORES)))
    rs = res.results
    g = lambda k: [np.asarray(r[k], dtype=np.float32) for r in rs]
    y_prompt = np.stack(g("y"), 0)
    y_sample = np.concatenate(g("ys"), 0).reshape(NCORES * NS, 1, D)
    new_pool_prompt = np.stack(g("npp"), 0)[None]
    new_conv_prompt = np.stack(g("ncp"), 0)[None]
    mem_k_prompt = np.stack(g("mk"), 1).reshape(2, NCORES, NMEM, 4, 256)
    mem_v_prompt = np.stack(g("mv"), 1).reshape(2, NCORES, NMEM, 4, 256)
    new_pool_sample = np.concatenate(g("nps"), 0)[None]
    new_conv_sample = np.concatenate(g("ncs"), 0)[None]
    return (y_prompt, y_sample, new_pool_prompt, new_conv_prompt, mem_k_prompt, mem_v_prompt,
            new_pool_sample, new_conv_sample)
```
